# Optimizing a Trainium2 kernel written in Bass

```python
import math
import jax
import jax.numpy as jnp
from jax import lax
import numpy as np

D_MODEL = 2048
BATCH = 4
SEQ = 2048
DEPTH = 2

N_MIXERS = 2
N_ATTN_LAYERS = (DEPTH + 1) // 2
N_GDN_LAYERS = DEPTH // 2
D_FF = 256 * ((8 * D_MODEL // 3 + 255) // 256)
RMS_EPS = 1e-6
DA_HEAD_DIM = 128
DA_HEADS = D_MODEL // (2 * DA_HEAD_DIM)
ROPE_THETA = 10000.0
Q_BLOCK = 128
GDN_HEAD_DIM = 128
GDN_K_HEADS = D_MODEL // GDN_HEAD_DIM
GDN_V_HEADS = 2 * GDN_K_HEADS
GDN_KEY_DIM = GDN_K_HEADS * GDN_HEAD_DIM
GDN_VAL_DIM = GDN_V_HEADS * GDN_HEAD_DIM
GDN_CONV_DIM = 2 * GDN_KEY_DIM + GDN_VAL_DIM
GDN_CONV_WIDTH = 4
GDN_IN_DIM = GDN_CONV_DIM + GDN_VAL_DIM + 2 * GDN_V_HEADS
CHUNK = 64

kernel_name = 'hybrid_diffattn_gated_deltanet_macaron'


def rms_norm(x, w):
    xf = x.astype(jnp.float32)
    y = xf * lax.rsqrt(jnp.mean(xf * xf, axis=-1, keepdims=True) + RMS_EPS)
    return (y * w.astype(jnp.float32)).astype(x.dtype)


def swiglu(x, w_gate, w_up, w_down):
    return (jax.nn.silu(x @ w_gate) * (x @ w_up)) @ w_down


def rope_tables(seq, dim):
    inv = 1.0 / (ROPE_THETA ** (jnp.arange(0, dim, 2, dtype=jnp.float32) / dim))
    ang = jnp.arange(seq, dtype=jnp.float32)[:, None] * inv[None, :]
    ang = jnp.concatenate([ang, ang], axis=-1)
    return jnp.cos(ang), jnp.sin(ang)


def apply_rope(x, cos, sin):
    xf = x.astype(jnp.float32)
    half = xf.shape[-1] // 2
    rot = jnp.concatenate([-xf[..., half:], xf[..., :half]], axis=-1)
    return (xf * cos[None, :, None, :] + rot * sin[None, :, None, :]).astype(x.dtype)


def diff_attention(h, w_qkv, lq1, lk1, lq2, lk2, subln, w_o, lambda_init):
    B, S, _ = h.shape
    q, k, v = jnp.split(h @ w_qkv, 3, axis=-1)
    q = q.reshape(B, S, 2 * DA_HEADS, DA_HEAD_DIM)
    k = k.reshape(B, S, 2 * DA_HEADS, DA_HEAD_DIM)
    v = v.reshape(B, S, DA_HEADS, 2 * DA_HEAD_DIM)
    cos, sin = rope_tables(S, DA_HEAD_DIM)
    q = apply_rope(q, cos, sin)
    k = apply_rope(k, cos, sin)
    lam = (jnp.exp(jnp.sum(lq1.astype(jnp.float32) * lk1.astype(jnp.float32)))
           - jnp.exp(jnp.sum(lq2.astype(jnp.float32) * lk2.astype(jnp.float32))) + lambda_init)
    scale = DA_HEAD_DIM ** -0.5
    k_t = jnp.transpose(k, (0, 2, 1, 3))
    v_t = jnp.transpose(v, (0, 2, 1, 3))
    n_blocks = S // Q_BLOCK
    q_blocks = jnp.transpose(q.reshape(B, n_blocks, Q_BLOCK, 2 * DA_HEADS, DA_HEAD_DIM), (1, 0, 3, 2, 4))
    key_pos = jnp.arange(S)
    neg = jnp.finfo(jnp.float32).min

    def one_block(args):
        qb, start = args
        s = jnp.einsum('bhqd,bhkd->bhqk', qb, k_t).astype(jnp.float32) * scale
        q_pos = start + jnp.arange(Q_BLOCK)
        s = jnp.where(key_pos[None, :] <= q_pos[:, None], s, neg)
        p = jax.nn.softmax(s, axis=-1).reshape(B, DA_HEADS, 2, Q_BLOCK, S)
        a = p[:, :, 0] - lam * p[:, :, 1]
        return jnp.einsum('bhqk,bhkd->bhqd', a.astype(v_t.dtype), v_t)

    o = lax.map(one_block, (q_blocks, jnp.arange(n_blocks) * Q_BLOCK))
    o = jnp.transpose(o, (1, 0, 3, 2, 4)).reshape(B, S, DA_HEADS, 2 * DA_HEAD_DIM)
    o = rms_norm(o, subln) * (1.0 - lambda_init)
    return o.reshape(B, S, D_MODEL) @ w_o


def causal_depthwise_conv(x, w):
    return lax.conv_general_dilated(
        x, w[:, None, :].astype(x.dtype), window_strides=(1,),
        padding=[(GDN_CONV_WIDTH - 1, 0)], dimension_numbers=('NWC', 'WIO', 'NWC'),
        feature_group_count=x.shape[-1])


def l2_normalize(x):
    return x * lax.rsqrt(jnp.sum(x * x, axis=-1, keepdims=True) + 1e-6)


def chunk_gated_delta_rule(q, k, v, g, beta):
    B, T, H, dk = q.shape
    dv = v.shape[-1]
    n = T // CHUNK

    def chunks(t):
        t = jnp.moveaxis(t, 2, 1)
        return t.reshape((B, H, n, CHUNK) + t.shape[3:])

    q = chunks(q) * (dk ** -0.5)
    k, v, g, beta = chunks(k), chunks(v), chunks(g), chunks(beta)
    g = jnp.cumsum(g, axis=-1)
    tril = jnp.tril(jnp.ones((CHUNK, CHUNK), dtype=bool))
    strict = jnp.tril(jnp.ones((CHUNK, CHUNK), dtype=bool), k=-1)
    gdiff = g[..., :, None] - g[..., None, :]
    decay = jnp.where(tril, jnp.exp(jnp.where(tril, gdiff, 0.0)), 0.0)
    k_beta = k * beta[..., None]
    v_beta = v * beta[..., None]
    a_mat = jnp.where(strict, jnp.einsum('bhncd,bhnsd->bhncs', k_beta, k) * decay, 0.0)
    eye = jnp.eye(CHUNK, dtype=jnp.float32)
    t_mat = lax.linalg.triangular_solve(a_mat + eye, jnp.broadcast_to(eye, a_mat.shape),
                                        left_side=True, lower=True, unit_diagonal=True)
    u = jnp.einsum('bhncs,bhnsd->bhncd', t_mat, v_beta)
    w = jnp.einsum('bhncs,bhnsd->bhncd', t_mat, k_beta * jnp.exp(g)[..., None])
    qk = jnp.where(tril, jnp.einsum('bhncd,bhnsd->bhncs', q, k) * decay, 0.0)
    q_g = q * jnp.exp(g)[..., None]
    k_g = k * jnp.exp(g[..., -1:] - g)[..., None]
    g_last = jnp.exp(g[..., -1])
    xs = tuple(jnp.moveaxis(t, 2, 0) for t in (q_g, k_g, u, w, qk, g_last))

    def step(state, inp):
        q_c, k_c, u_c, w_c, qk_c, gl = inp
        v_new = u_c - jnp.einsum('bhcd,bhde->bhce', w_c, state)
        o = jnp.einsum('bhcd,bhde->bhce', q_c, state) + jnp.einsum('bhcs,bhse->bhce', qk_c, v_new)
        state = state * gl[..., None, None] + jnp.einsum('bhcd,bhce->bhde', k_c, v_new)
        return state, o

    _, o = lax.scan(step, jnp.zeros((B, H, dk, dv), jnp.float32), xs)
    return jnp.transpose(o, (1, 0, 3, 2, 4)).reshape(B, T, H, dv)


def gated_deltanet(h, w_in, conv_w, a_log, dt_bias, norm_w, w_o):
    B, S, _ = h.shape
    proj = h @ w_in
    qkv, z, b, a = jnp.split(proj, [GDN_CONV_DIM, GDN_CONV_DIM + GDN_VAL_DIM,
                                    GDN_CONV_DIM + GDN_VAL_DIM + GDN_V_HEADS], axis=-1)
    qkv = jax.nn.silu(causal_depthwise_conv(qkv, conv_w)).astype(jnp.float32)
    q, k, v = jnp.split(qkv, [GDN_KEY_DIM, 2 * GDN_KEY_DIM], axis=-1)
    q = l2_normalize(q.reshape(B, S, GDN_K_HEADS, GDN_HEAD_DIM))
    k = l2_normalize(k.reshape(B, S, GDN_K_HEADS, GDN_HEAD_DIM))
    v = v.reshape(B, S, GDN_V_HEADS, GDN_HEAD_DIM)
    rep = GDN_V_HEADS // GDN_K_HEADS
    q = jnp.repeat(q, rep, axis=2)
    k = jnp.repeat(k, rep, axis=2)
    beta = jax.nn.sigmoid(b.astype(jnp.float32))
    g = -jnp.exp(a_log.astype(jnp.float32)) * jax.nn.softplus(
        a.astype(jnp.float32) + dt_bias.astype(jnp.float32))
    o = chunk_gated_delta_rule(q, k, v, g, beta)
    zf = z.reshape(B, S, GDN_V_HEADS, GDN_HEAD_DIM).astype(jnp.float32)
    o = (o * lax.rsqrt(jnp.mean(o * o, axis=-1, keepdims=True) + RMS_EPS)
         * norm_w.astype(jnp.float32) * jax.nn.silu(zf))
    return o.astype(h.dtype).reshape(B, S, GDN_VAL_DIM) @ w_o


def setup_inputs(seed: int = 0) -> dict:
    key = jax.random.key(seed)
    ks = iter(jax.random.split(key, 32))

    def nrm(shape, scale):
        return scale * jax.random.normal(next(ks), shape, jnp.float32)

    def gain(shape):
        return 1.0 + 0.05 * jax.random.normal(next(ks), shape, jnp.float32)

    D, F = D_MODEL, D_FF
    return {
        'x': nrm((BATCH, SEQ, D), 1.0),
        'ffn1_norm': gain((DEPTH, D)),
        'ffn1_w_gate': nrm((DEPTH, D, F), D ** -0.5),
        'ffn1_w_up': nrm((DEPTH, D, F), D ** -0.5),
        'ffn1_w_down': nrm((DEPTH, F, D), F ** -0.5),
        'mix_norm': gain((DEPTH, D)),
        'ffn2_norm': gain((DEPTH, D)),
        'ffn2_w_gate': nrm((DEPTH, D, F), D ** -0.5),
        'ffn2_w_up': nrm((DEPTH, D, F), D ** -0.5),
        'ffn2_w_down': nrm((DEPTH, F, D), F ** -0.5),
        'da_w_qkv': nrm((N_ATTN_LAYERS, D, 3 * D), D ** -0.5),
        'da_lambda_q1': nrm((N_ATTN_LAYERS, DA_HEAD_DIM), 0.1),
        'da_lambda_k1': nrm((N_ATTN_LAYERS, DA_HEAD_DIM), 0.1),
        'da_lambda_q2': nrm((N_ATTN_LAYERS, DA_HEAD_DIM), 0.1),
        'da_lambda_k2': nrm((N_ATTN_LAYERS, DA_HEAD_DIM), 0.1),
        'da_subln': gain((N_ATTN_LAYERS, 2 * DA_HEAD_DIM)),
        'da_w_o': nrm((N_ATTN_LAYERS, D, D), D ** -0.5),
        'gdn_w_in': nrm((N_GDN_LAYERS, D, GDN_IN_DIM), D ** -0.5),
        'gdn_conv_w': nrm((N_GDN_LAYERS, GDN_CONV_WIDTH, GDN_CONV_DIM), GDN_CONV_WIDTH ** -0.5),
        'gdn_a_log': jnp.log(jax.random.uniform(next(ks), (N_GDN_LAYERS, GDN_V_HEADS), jnp.float32, 1.0, 16.0)),
        'gdn_dt_bias': nrm((N_GDN_LAYERS, GDN_V_HEADS), 0.5),
        'gdn_norm': gain((N_GDN_LAYERS, GDN_HEAD_DIM)),
        'gdn_w_o': nrm((N_GDN_LAYERS, GDN_VAL_DIM, D), GDN_VAL_DIM ** -0.5),
        'final_norm': gain((D,)),
    }


def reference(x, ffn1_norm, ffn1_w_gate, ffn1_w_up, ffn1_w_down, mix_norm,
              ffn2_norm, ffn2_w_gate, ffn2_w_up, ffn2_w_down,
              da_w_qkv, da_lambda_q1, da_lambda_k1, da_lambda_q2, da_lambda_k2, da_subln, da_w_o,
              gdn_w_in, gdn_conv_w, gdn_a_log, gdn_dt_bias, gdn_norm, gdn_w_o, final_norm):
    h = x
    for i in range(DEPTH):
        h = h + 0.5 * swiglu(rms_norm(h, ffn1_norm[i]), ffn1_w_gate[i], ffn1_w_up[i], ffn1_w_down[i])
        hn = rms_norm(h, mix_norm[i])
        j = i // N_MIXERS
        if i % N_MIXERS == 0:
            lambda_init = 0.8 - 0.6 * math.exp(-0.3 * i)
            mix = diff_attention(hn, da_w_qkv[j], da_lambda_q1[j], da_lambda_k1[j],
                                 da_lambda_q2[j], da_lambda_k2[j], da_subln[j], da_w_o[j], lambda_init)
        else:
            mix = gated_deltanet(hn, gdn_w_in[j], gdn_conv_w[j], gdn_a_log[j], gdn_dt_bias[j],
                                 gdn_norm[j], gdn_w_o[j])
        h = h + mix
        h = h + 0.5 * swiglu(rms_norm(h, ffn2_norm[i]), ffn2_w_gate[i], ffn2_w_up[i], ffn2_w_down[i])
    return rms_norm(h, final_norm)
```

```python
import math
from contextlib import ExitStack

import numpy as np

import concourse.bass as bass
import concourse.mybir as mybir
from concourse.bass_utils import run_bass_kernel_spmd

F32 = mybir.dt.float32
BF16 = mybir.dt.bfloat16
AF = mybir.ActivationFunctionType
ALU = mybir.AluOpType
AX = mybir.AxisListType

ENGS = ("pe", "act", "dve", "pool", "sp")
SEM_EPOCH = 30000

D_MODEL = 2048
D_FF = 5632
SEQ = 2048
BATCH = 4
RMS_EPS = 1e-6


class Tok:
    def __init__(self, name, space="sb"):
        self.name = name
        self.space = space
        self.writers = {}
        self.dma_writers = []
        self.readers = {}
        self.dma_readers = []
        self.group_deps = []
        self.reading = True
        self.dma_sem = None
        self.dma_cnt = 0


class Op:
    __slots__ = ("eng", "idx", "emit", "is_dma", "dma_sem", "dma_val", "waits", "dwaits",
                 "clock", "dknow", "signal", "sigval", "inc")

    def __init__(self, eng, emit, is_dma):
        self.eng = eng
        self.emit = emit
        self.is_dma = is_dma
        self.idx = -1
        self.dma_sem = None
        self.dma_val = 0
        self.waits = []
        self.dwaits = []
        self.clock = None
        self.dknow = None
        self.signal = False
        self.sigval = None
        self.inc = 16


class Prog:
    def __init__(self, nc, stack):
        self.nc = nc
        self.stack = stack
        self.streams = {e: [] for e in ENGS}
        self.nidx = {e: 0 for e in ENGS}
        self.clock = {e: {} for e in ENGS}
        self.dknow = {e: {} for e in ENGS}
        self.toks = {}
        self.nsem = 0
        self.out_dmas = []
        self.outer = stack
        self.phase = 0
        self.phase_toks = []
        self.free_sems = []
        self.all_sems = {}
        self.esems = {e: [] for e in ENGS}
        self.sigcount = {e: 0 for e in ENGS}

    def sem(self, name):
        self.nsem += 1
        return self.outer.enter_context(self.nc.semaphore(name))

    def sb(self, name, shape, dtype, ntok=1):
        if self.phase:
            name = f"{name}_p{self.phase}"
        h = self.stack.enter_context(self.nc.sbuf_tensor(name, list(shape), dtype))
        self.toks[h.name] = [Tok(f"{name}.{i}") for i in range(ntok)]
        if self.stack is not self.outer:
            self.phase_toks.extend(self.toks[h.name])
        return h

    def ps(self, name, shape, dtype=F32):
        h = self.outer.enter_context(self.nc.psum_tensor(name, list(shape), dtype))
        self.toks[h.name] = [Tok(name, "ps")]
        return h

    def dram(self, name, shape, dtype, kind="Internal", track=True):
        h = self.nc.dram_tensor(name, list(shape), dtype, kind=kind)
        if track:
            self.toks[h.name] = [Tok(name, "dram")]
        return h

    def tok(self, h, i=None):
        t = self.toks[h.name]
        return t if i is None else [t[i]]

    def _toks_of(self, aps):
        out = []
        for a in aps:
            if a is None or isinstance(a, (int, float)):
                continue
            if isinstance(a, Tok):
                out.append(a)
                continue
            if isinstance(a, (list, tuple)):
                out.extend(self._toks_of(a))
                continue
            nm = a.tensor.name if hasattr(a, "tensor") else a.name
            if nm in self.toks:
                out.extend(self.toks[nm])
        seen = set()
        res = []
        for t in out:
            if id(t) not in seen:
                seen.add(id(t))
                res.append(t)
        return res

    def op(self, eng, emit, reads=(), writes=(), is_dma=False, ordered=False, inc=16):
        X = Op(eng, emit, is_dma)
        R = self._toks_of(reads)
        W = self._toks_of(writes)
        deps = []
        for t in R:
            for d in t.writers.values():
                deps.append((d, "RAW"))
            for d in t.dma_writers:
                deps.append((d, "RAW"))
        for t in R:
            if t.space == "ps":
                for e2, d in t.readers.items():
                    if e2 != eng:
                        deps.append((d, "RAR"))
        for t in W:
            other = (any(e != eng for e in t.writers) or bool(t.dma_writers)) if not is_dma else bool(t.writers)
            if t.reading or ordered or other:
                gd = [(d, "WAW") for d in t.writers.values()] + [(d, "WAW") for d in t.dma_writers]
                gd += [(d, "WAR") for d in t.readers.values()] + [(d, "WAR") for d in t.dma_readers]
                t.group_deps = gd
                t.writers = {}
                t.dma_writers = []
                t.readers = {}
                t.dma_readers = []
                t.reading = False
            deps.extend(t.group_deps)
        wset = set(id(t) for t in W)
        for t in R:
            if id(t) in wset:
                continue
            t.reading = True
            if is_dma:
                t.dma_readers.append(X)
            else:
                t.readers[eng] = X
        for t in W:
            if is_dma:
                t.dma_writers.append(X)
            else:
                t.writers[eng] = X
        for t in R:
            if id(t) in wset:
                t.reading = True

        clock = self.clock[eng]
        dknow = self.dknow[eng]
        dmax = {}
        for (D, kind) in deps:
            if D.is_dma and D is not X:
                if dmax.get(D.dma_sem, 0) < D.dma_val:
                    dmax[D.dma_sem] = D.dma_val
        for (D, kind) in deps:
            if D is X:
                continue
            if D.is_dma:
                if D.dma_val < dmax[D.dma_sem]:
                    continue
                if dknow.get(D.dma_sem, 0) >= D.dma_val:
                    continue
                X.dwaits.append((D.dma_sem, D.dma_val))
                dknow[D.dma_sem] = D.dma_val
            else:
                if D.eng == eng:
                    if eng == "pe" or kind == "WAR":
                        continue
                if clock.get(D.eng, -1) >= D.idx:
                    continue
                X.waits.append(D)
                D.signal = True
                clock[D.eng] = D.idx
            for e, i in D.clock.items():
                if clock.get(e, -1) < i:
                    clock[e] = i
            for s, v in D.dknow.items():
                if dknow.get(s, 0) < v:
                    dknow[s] = v
        X.clock = dict(clock)
        X.dknow = dict(dknow)
        if is_dma:
            cand = [t for t in W if t.space == "sb"] or [t for t in R if t.space == "sb"] or (W + R)
            owner = cand[0]
            if owner.dma_sem is None:
                if self.free_sems:
                    owner.dma_sem, owner.dma_cnt = self.free_sems.pop()
                else:
                    owner.dma_sem = self.sem("d_" + owner.name.replace(".", "_"))
            owner.dma_cnt += inc
            X.dma_sem = owner.dma_sem
            X.dma_val = owner.dma_cnt
            X.inc = inc
            self.all_sems[owner.dma_sem] = owner.dma_cnt
        else:
            X.idx = self.nidx[eng]
            self.nidx[eng] += 1
        self.streams[eng].append(X)
        return X

    def matmul(self, out, lhsT, rhs, start=True, stop=True, **kw):
        return self.op("pe", lambda e: e.matmul(out, lhsT, rhs, start=start, stop=stop, **kw),
                       reads=[lhsT, rhs], writes=[out])

    def transpose(self, out, in_, ident):
        return self.op("pe", lambda e: e.transpose(out, in_, ident), reads=[in_, ident], writes=[out])

    def act(self, out, in_, func, bias=None, scale=1.0, accum_out=None, reads=None, writes=None):
        kw = {}
        if bias is not None:
            kw["bias"] = bias
        if accum_out is not None:
            kw["accum_out"] = accum_out
        r = [in_, bias, scale] if reads is None else reads
        w = [out, accum_out] if writes is None else writes
        return self.op("act", lambda e: e.activation(out, in_, func, scale=scale, **kw), reads=r, writes=w)

    def tt(self, eng, out, in0, in1, op, reads=None, writes=None):
        r = [in0, in1] if reads is None else reads
        w = [out] if writes is None else writes
        return self.op(eng, lambda e: e.tensor_tensor(out, in0, in1, op), reads=r, writes=w)

    def ts(self, eng, out, in0, s1, s2, op0, op1=None, accum_out=None, reads=None, writes=None):
        kw = {}
        if op1 is not None:
            kw["op1"] = op1
        if accum_out is not None:
            kw["accum_out"] = accum_out
        r = [in0, s1, s2] if reads is None else reads
        w = [out, accum_out] if writes is None else writes
        return self.op(eng, lambda e: e.tensor_scalar(out, in0, s1, s2, op0, **kw), reads=r, writes=w)

    def stt(self, eng, out, in0, scalar, in1, op0, op1, reads=None, writes=None):
        r = [in0, scalar, in1] if reads is None else reads
        w = [out] if writes is None else writes
        return self.op(eng, lambda e: e.scalar_tensor_tensor(out, in0, scalar, in1, op0, op1), reads=r, writes=w)

    def copy(self, eng, out, in_, reads=None, writes=None):
        r = [in_] if reads is None else reads
        w = [out] if writes is None else writes
        if eng == "act":
            return self.op("act", lambda e: e.copy(out, in_), reads=r, writes=w)
        return self.op(eng, lambda e: e.tensor_copy(out, in_), reads=r, writes=w)

    def reduce(self, eng, out, in_, op, axis=AX.X, reads=None, writes=None):
        r = [in_] if reads is None else reads
        w = [out] if writes is None else writes
        return self.op(eng, lambda e: e.tensor_reduce(out, in_, axis, op), reads=r, writes=w)

    def memset(self, eng, ap, val):
        return self.op(eng, lambda e: e.memset(ap, val), reads=[], writes=[ap])

    def dma(self, q, out, in_, reads=None, writes=None, **kw):
        r = [in_] if reads is None else reads
        w = [out] if writes is None else writes
        X = self.op(q, lambda e: e.dma_start(out=out, in_=in_, **kw), reads=r, writes=w, is_dma=True)
        return X

    def finish(self, out_toks):
        return self.op("sp", lambda e: e.nop(), reads=out_toks, writes=[])

    def collective(self, kind, in_ap, out_ap, groups):
        return self.op("pool", lambda e: e.collective_compute(kind, ALU.bypass, replica_groups=groups,
                                                              ins=[in_ap], outs=[out_ap]),
                       reads=[in_ap], writes=[out_ap], is_dma=True, inc=1)

    def emit(self):
        nc = self.nc
        for e in ENGS:
            for o in self.streams[e]:
                if o.signal:
                    self.sigcount[e] += 1
                    o.sigval = self.sigcount[e]
            need = max(1, (self.sigcount[e] + SEM_EPOCH - 1) // SEM_EPOCH)
            while len(self.esems[e]) < need:
                self.esems[e].append(self.sem(f"e_{e}_{len(self.esems[e])}"))
        streams = self.streams
        esems = self.esems

        def body(ename):
            def f(eng):
                for o in streams[ename]:
                    for D in o.waits:
                        v = D.sigval - 1
                        eng.wait_ge(esems[D.eng][v // SEM_EPOCH], v % SEM_EPOCH + 1)
                    for (s, v) in o.dwaits:
                        eng.wait_ge(s, v)
                    ins = o.emit(eng)
                    if o.is_dma:
                        if o.inc == 16:
                            ins.then_inc(o.dma_sem, 16)
                        else:
                            ins.then_inc(o.dma_sem)
                    elif o.signal:
                        v = o.sigval - 1
                        ins.then_inc(esems[ename][v // SEM_EPOCH], 1)
            return f

        with nc.Block() as block:
            block.tensor(body("pe"))
            block.scalar(body("act"))
            block.vector(body("dve"))
            block.gpsimd(body("pool"))
            block.sync(body("sp"))
        self.streams = {e: [] for e in ENGS}

    def end_phase(self):
        X = Op("sp", lambda e: e.nop(), False)
        dk = self.dknow["sp"]
        for s_, v in self.all_sems.items():
            if dk.get(s_, 0) < v:
                X.dwaits.append((s_, v))
                dk[s_] = v
        X.clock = dict(self.clock["sp"])
        X.dknow = dict(dk)
        X.idx = self.nidx["sp"]
        self.nidx["sp"] += 1
        self.streams["sp"].append(X)
        self.emit()
        full = {e: self.nidx[e] - 1 for e in ENGS}
        for e in ENGS:
            self.clock[e] = dict(full)
            self.dknow[e] = dict(self.all_sems)
        for t in self.phase_toks:
            if t.dma_sem is not None:
                self.free_sems.append((t.dma_sem, t.dma_cnt))
                t.dma_sem = None
        self.phase_toks = []
        self.phase += 1


class Ctx:
    def __init__(self, P, ntt, nh=None, xT_chunks=16, light=False):
        self.P = P
        self.ntt = ntt
        nt = ntt * 128
        self.nt = nt
        if light:
            self.ident_f = P.sb("ident_f", [128, 128], F32)
            self.ident = P.sb("ident", [128, 128], BF16)
            self.bank = [P.ps(f"bank{i}", [128, 512], F32) for i in range(8)]
            P.memset("pool", self.ident_f[:], 0.0)
            P.op("pool", lambda e: e.affine_select(out=self.ident_f[:], in_=self.ident_f[:],
                                                   compare_op=ALU.not_equal, fill=1.0, base=0,
                                                   pattern=[[-1, 128]], channel_multiplier=1),
                 reads=[self.ident_f], writes=[self.ident_f])
            P.copy("dve", self.ident[:], self.ident_f[:])
            return
        self.h = [P.sb(f"h{t}", [128, D_MODEL], F32, ntok=4) for t in range(ntt if nh is None else nh)]
        self.xT = P.sb("xT", [128, xT_chunks, nt], BF16)
        self.xn = [P.sb(f"xn{i}", [128, D_MODEL], BF16) for i in range(2)]
        self.junk = P.sb("junk", [128, D_MODEL], BF16)
        self.ss = [P.sb(f"ss{i}", [128, 1], F32) for i in range(2)]
        self.rstd = [P.sb(f"rstd{i}", [128, 1], F32) for i in range(2)]
        self.wn = P.sb("wn_sb", [128, D_MODEL], F32)
        self.ident_f = P.sb("ident_f", [128, 128], F32)
        self.ident = P.sb("ident", [128, 128], BF16)
        self.bank = [P.ps(f"bank{i}", [128, 512], F32) for i in range(8)]
        self.rr = 0
        P.memset("pool", self.ident_f[:], 0.0)
        P.op("pool", lambda e: e.affine_select(out=self.ident_f[:], in_=self.ident_f[:],
                                               compare_op=ALU.not_equal, fill=1.0, base=0,
                                               pattern=[[-1, 128]], channel_multiplier=1),
             reads=[self.ident_f], writes=[self.ident_f])
        P.copy("dve", self.ident[:], self.ident_f[:])


def alloc_norm(P, C, xT_chunks=16, xT=True):
    if xT:
        C.xT = P.sb("xT", [128, xT_chunks, C.nt], BF16)
    C.xn = [P.sb(f"xn{i}", [128, D_MODEL], BF16) for i in range(2)]
    C.junk = P.sb("junk", [128, D_MODEL], BF16)
    C.ss = [P.sb(f"ss{i}", [128, 1], F32) for i in range(2)]
    C.rstd = [P.sb(f"rstd{i}", [128, 1], F32) for i in range(2)]
    C.wn = P.sb("wn_sb", [128, D_MODEL], F32)


def rmsnorm_to_xT(P, C, wnorm_ap, banks, extra=None):
    nc = P.nc
    P.dma("sp", C.wn[:], wnorm_ap.partition_broadcast(128))
    for t in range(C.ntt + (1 if extra is not None else 0)):
        i = t % 2
        h = C.h[t] if t < C.ntt else extra[0]
        P.memset("dve", C.ss[i][:], 0.0)
        P.act(C.junk[:], h[:], AF.Square, accum_out=C.ss[i][:])
        P.ts("dve", C.rstd[i][:], C.ss[i][:], 1.0 / D_MODEL, RMS_EPS, ALU.mult, ALU.add)
        P.act(C.rstd[i][:], C.rstd[i][:], AF.Sqrt)
        P.op("dve", lambda e, o=C.rstd[i]: e.reciprocal(o[:], o[:]), reads=[C.rstd[i]], writes=[C.rstd[i]])
        P.stt("dve", C.xn[i][:], h[:], C.rstd[i][:, 0:1], C.wn[:], ALU.mult, ALU.mult)
        for half in range(2):
            bk = banks[(2 * t + half) % len(banks)]
            tp = bk[:].bitcast(BF16)
            for j in range(8):
                dc = half * 8 + j
                P.transpose(tp[:, j * 128:(j + 1) * 128], C.xn[i][:, dc * 128:(dc + 1) * 128], C.ident[:])
            src = tp.rearrange("p (j n) -> p j n", j=8)
            if t < C.ntt:
                dst = C.xT[:, half * 8:(half + 1) * 8, t * 128:(t + 1) * 128]
            else:
                dst = extra[1][:, half * 8:(half + 1) * 8, :]
            if half == 0:
                P.copy("act", dst, src)
            else:
                P.copy("dve", dst, src)


def ffn(P, C, wnorm_ap, wg_ap, wu_ap, wd_ap, wbufs, G=2):
    nt = C.nt
    nth = nt // 512
    rmsnorm_to_xT(P, C, wnorm_ap, C.bank[4:8])
    wg_v = wg_ap.rearrange("(dc p) f -> p dc f", p=128)
    wu_v = wu_ap.rearrange("(dc p) f -> p dc f", p=128)
    wd_v = wd_ap.rearrange("(fc p) d -> p fc d", p=128)
    nfc = D_FF // 128
    ngroups = nfc // G
    gu_banks = C.bank[0:4]
    dn_banks = C.bank[4:8]
    gu_i = 0
    dn_i = 0
    for g in range(ngroups):
        wb = wbufs[g % len(wbufs)]
        f0 = g * G * 128
        P.dma("pool", wb["wg"][:], wg_v[:, :, f0:f0 + G * 128])
        P.dma("pool", wb["wu"][:], wu_v[:, :, f0:f0 + G * 128])
        P.dma("pool", wb["wd"][:], wd_v[:, g * G:(g + 1) * G, :])
        aT = wb["aT"]
        for fl in range(G):
            for th in range(nth):
                psg = gu_banks[gu_i % 4]
                psu = gu_banks[(gu_i + 1) % 4]
                gu_i += 2
                for dc in range(16):
                    P.matmul(psg[:], wb["wg"][:, dc, fl * 128:(fl + 1) * 128], C.xT[:, dc, th * 512:(th + 1) * 512],
                             start=(dc == 0), stop=(dc == 15))
                for dc in range(16):
                    P.matmul(psu[:], wb["wu"][:, dc, fl * 128:(fl + 1) * 128], C.xT[:, dc, th * 512:(th + 1) * 512],
                             start=(dc == 0), stop=(dc == 15))
                sg = wb["sg"][(fl * nth + th) % 2]
                P.act(sg[:], psg[:], AF.Silu)
                P.tt("dve", aT[:, fl, th * 512:(th + 1) * 512], psu[:], sg[:], ALU.mult)
        for t in range(C.ntt):
            for half in range(2):
                pd = [dn_banks[dn_i % 4], dn_banks[(dn_i + 1) % 4]]
                dn_i += 2
                for fl in range(G):
                    for j in range(2):
                        q = half * 2 + j
                        P.matmul(pd[j][:], aT[:, fl, t * 128:(t + 1) * 128], wb["wd"][:, fl, q * 512:(q + 1) * 512],
                                 start=(fl == 0), stop=(fl == G - 1))
                for j in range(2):
                    q = half * 2 + j
                    hq = C.h[t][:, q * 512:(q + 1) * 512]
                    tk = P.tok(C.h[t], q)
                    P.stt("dve", hq, pd[j][:], 0.5, hq, ALU.mult, ALU.add,
                          reads=[pd[j], tk], writes=[tk])


def make_wbufs(P, nt, G, nbuf, with_ffn=True):
    bufs = []
    for i in range(nbuf):
        b = {
            "wg": P.sb(f"wg{i}", [128, 16, G * 128], BF16),
            "wu": P.sb(f"wu{i}", [128, 16, G * 128], BF16),
            "wd": P.sb(f"wd{i}", [128, G, D_MODEL], BF16),
        }
        if with_ffn:
            b["aT"] = P.sb(f"aT{i}", [128, G, nt], BF16)
            b["sg"] = [P.sb(f"sg{i}_{k}", [128, 512], F32) for k in range(2)]
        bufs.append(b)
    return bufs


ADD_ENG = ["pool"]


class Ring:
    def __init__(self, P, name, shape, dtype, n):
        self.bufs = [P.sb(f"{name}{i}", shape, dtype) for i in range(n)]
        self.i = 0

    def next(self):
        b = self.bufs[self.i % len(self.bufs)]
        self.i += 1
        return b


def qkv_rope(P, C, wqkv_ap, cos_sb, sin_sb, rot_sb, qT_d, kT_d, v_d, wbufs, heads=range(8), do_qk=True, do_v=True, do_rope=True):
    nt = C.nt
    nth = nt // 512
    wv_ = wqkv_ap.rearrange("(dc p) f -> p dc f", p=128)
    st_b = Ring(P, "qk_b", [128, 512], BF16, 2)
    st_1 = Ring(P, "qk_t1", [128, 512], F32, 2)
    st_2 = Ring(P, "qk_t2", [128, 512], F32, 2)
    st_o = Ring(P, "qk_o", [128, 512], BF16, 3)
    st_v = Ring(P, "v_st", [128, C.ntt, 256], BF16, 2)
    qf = qT_d if callable(qT_d) else (lambda hd: qT_d[2 * hd:2 * hd + 2])
    kf = kT_d if callable(kT_d) else (lambda hd: kT_d[2 * hd:2 * hd + 2])
    v_list = v_d if isinstance(v_d, (list, tuple)) else [v_d]
    v_views = [v.rearrange("(t p) c -> p t c", p=128) for v in v_list]
    tpv = C.ntt // len(v_views)
    bi = 0
    for hd in heads:
        wb = wbufs[hd % len(wbufs)]
        wq = wb["wg"]
        wk = wb["wu"]
        wvv = wb["wd"][:].rearrange("p g (a b) -> p (g a) b", b=256)
        P.dma("pool", wq[:], wv_[:, :, hd * 256:(hd + 1) * 256])
        P.dma("pool", wk[:], wv_[:, :, 2048 + hd * 256:2048 + (hd + 1) * 256])
        P.dma("pool", wvv, wv_[:, :, 4096 + hd * 256:4096 + (hd + 1) * 256])
        for (w, dst) in (((wq, qf(hd)), (wk, kf(hd))) if do_qk else ()):
            for sub in range(2):
                for th in range(nth):
                    ps = C.bank[bi % 4]
                    psr = C.bank[4 + bi % 4]
                    bi += 1
                    for dc in range(16):
                        P.matmul(ps[:], w[:, dc, sub * 128:(sub + 1) * 128], C.xT[:, dc, th * 512:(th + 1) * 512],
                                 start=(dc == 0), stop=(dc == 15))
                    qb = st_b.next()
                    P.copy("act", qb[:], ps[:])
                    if not do_rope:
                        P.dma("sp", dst[sub, :, th * 512:(th + 1) * 512], qb[:])
                        continue
                    P.matmul(psr[:], rot_sb[:], qb[:], start=True, stop=True)
                    t1 = st_1.next()
                    t2 = st_2.next()
                    P.tt("dve", t1[:], ps[:], cos_sb[:, th * 512:(th + 1) * 512], ALU.mult)
                    P.tt("dve", t2[:], psr[:], sin_sb[:, th * 512:(th + 1) * 512], ALU.mult)
                    qo = st_o.next()
                    P.tt(ADD_ENG[0], qo[:], t1[:], t2[:], ALU.add)
                    P.dma("sp", dst[sub, :, th * 512:(th + 1) * 512], qo[:])
        vst = st_v.next()
        for t in (range(C.ntt) if do_v else ()):
            ps = C.bank[bi % 4]
            bi += 1
            for dc in range(16):
                P.matmul(ps[:, 0:256], C.xT[:, dc, t * 128:(t + 1) * 128], wvv[:, dc, :],
                         start=(dc == 0), stop=(dc == 15))
            P.copy("act", vst[:, t, :], ps[:, 0:256])
        if do_v:
            for vi, vv in enumerate(v_views):
                P.dma("sp", vv[:, :, hd * 256:(hd + 1) * 256], vst[:, vi * tpv:(vi + 1) * tpv, :])


def attn_core(P, C, qT_d, kTp_d, kTo_d, vp_d, vo_d, pbias_sb, nlam_sb, sublnw_sb, nkt_prev, nkt_own):
    nt = C.nt
    scale = 128.0 ** -0.5
    nq = nt // 256
    qT = P.sb("a_qT", [128, 2, nt], BF16)
    kT = P.sb("a_kT", [128, 2, (nkt_prev + nkt_own) * 128], BF16)
    V = P.sb("a_V", [128, nkt_prev + nkt_own, 258], BF16)
    pTr = Ring(P, "a_pT", [128, 512], BF16, 3)
    msk = [P.sb(f"a_msk{r}", [128, 2, 256], BF16) for r in range(2)]
    o_sb = Ring(P, "a_o", [128, 256], F32, 2)
    on_sb = Ring(P, "a_on", [128, 256], BF16, 2)
    sq_sb = P.sb("a_sq", [128, 256], F32)
    sm = Ring(P, "a_sm", [128, 8], F32, 2)
    for r in range(2):
        P.memset("pool", msk[r][:], 1.0)
        P.op("pool", lambda e, r=r: e.affine_select(out=msk[r][:], in_=msk[r][:], compare_op=ALU.is_ge, fill=0.0,
                                                    base=-128 * r, pattern=[[0, 2], [1, 256]],
                                                    channel_multiplier=-1),
             reads=[msk[r]], writes=[msk[r]])
    P.memset("pool", V[:, :, 256:258], 1.0)
    qf = qT_d if callable(qT_d) else (lambda hd: qT_d[2 * hd:2 * hd + 2])
    kpf = kTp_d if callable(kTp_d) else (lambda hd: kTp_d[2 * hd:2 * hd + 2])
    kof = kTo_d if callable(kTo_d) else (lambda hd: kTo_d[2 * hd:2 * hd + 2])
    vp_vs = [v.rearrange("(t p) c -> p t c", p=128) for v in (vp_d if isinstance(vp_d, (list, tuple)) else [vp_d])]
    vo_vs = [v.rearrange("(t p) c -> p t c", p=128) for v in (vo_d if isinstance(vo_d, (list, tuple)) else [vo_d])]
    s_banks = C.bank[0:3]
    o_banks = C.bank[3:7]
    t_bank = C.bank[7]
    si = 0
    for hd in range(8):
        P.dma("sp", qT[:], qf(hd).rearrange("s p t -> p s t"))
        if nkt_prev:
            P.dma("act", kT[:, :, 0:nkt_prev * 128], kpf(hd).rearrange("s p t -> p s t"))
            n1 = nkt_prev // len(vp_vs)
            for vi, vv in enumerate(vp_vs):
                P.dma("sp", V[:, vi * n1:(vi + 1) * n1, 0:256], vv[:, :, hd * 256:(hd + 1) * 256])
        P.dma("act", kT[:, :, nkt_prev * 128:], kof(hd).rearrange("s p t -> p s t"))
        n2 = nkt_own // len(vo_vs)
        for vi, vv in enumerate(vo_vs):
            P.dma("sp", V[:, nkt_prev + vi * n2:nkt_prev + (vi + 1) * n2, 0:256], vv[:, :, hd * 256:(hd + 1) * 256])
        for qb in range(nq):
            kts = [(k, "prev") for k in range(nkt_prev)]
            kts += [(nkt_prev + k, "full") for k in range(2 * qb)]
            kts += [(nkt_prev + 2 * qb, "d0"), (nkt_prev + 2 * qb + 1, "d1")]
            for i, (kt, kind) in enumerate(kts):
                psS = s_banks[si % 3]
                si += 1
                for sh in range(2):
                    P.matmul(psS[:, sh * 256:(sh + 1) * 256], kT[:, sh, kt * 128:(kt + 1) * 128],
                             qT[:, sh, qb * 256:(qb + 1) * 256], start=True, stop=True)
                pT = pTr.next()
                if kind == "prev":
                    P.act(pT[:], psS[:], AF.Exp, bias=pbias_sb[:, 0:1], scale=scale)
                else:
                    P.act(pT[:], psS[:], AF.Exp, scale=scale)
                if kind in ("d0", "d1"):
                    m = msk[0] if kind == "d0" else msk[1]
                    P.tt("pool", pT[:], pT[:], m[:].rearrange("p s q -> p (s q)"), ALU.mult)
                for qs in range(2):
                    for sh in range(2):
                        P.matmul(o_banks[qs * 2 + sh][:, 0:257],
                                 pT[:, sh * 256 + qs * 128:sh * 256 + (qs + 1) * 128], V[:, kt, 0:257],
                                 start=(i == 0), stop=(i == len(kts) - 1))
            for qs in range(2):
                O1 = o_banks[qs * 2]
                O2 = o_banks[qs * 2 + 1]
                s = sm.next()
                o = o_sb.next()
                P.op("dve", lambda e, s=s, O1=O1: e.reciprocal(s[:, 0:1], O1[:, 256:257]), reads=[O1], writes=[s])
                P.op("dve", lambda e, s=s, O2=O2: e.reciprocal(s[:, 1:2], O2[:, 256:257]), reads=[O2, s], writes=[s])
                P.tt("dve", s[:, 2:3], s[:, 1:2], nlam_sb[:, 0:1], ALU.mult)
                P.ts("dve", o[:], O1[:, 0:256], s[:, 0:1], None, ALU.mult)
                P.stt("dve", o[:], O2[:, 0:256], s[:, 2:3], o[:], ALU.mult, ALU.add)
                P.memset("dve", s[:, 3:4], 0.0)
                P.act(sq_sb[:], o[:], AF.Square, accum_out=s[:, 3:4])
                P.ts("dve", s[:, 4:5], s[:, 3:4], 1.0 / 256.0, RMS_EPS, ALU.mult, ALU.add)
                P.act(s[:, 5:6], s[:, 4:5], AF.Sqrt)
                P.op("dve", lambda e, s=s: e.reciprocal(s[:, 6:7], s[:, 5:6]), reads=[s], writes=[s])
                on = on_sb.next()
                P.stt("dve", on[:], o[:], s[:, 6:7], sublnw_sb[:], ALU.mult, ALU.mult)
                tp = t_bank[:].bitcast(BF16)
                tt_i = qb * 2 + qs
                for j in range(2):
                    P.transpose(tp[:, j * 128:(j + 1) * 128], on[:, j * 128:(j + 1) * 128], C.ident[:])
                P.copy("act", C.xT[:, 2 * hd:2 * hd + 2, tt_i * 128:(tt_i + 1) * 128],
                       tp[:, 0:256].rearrange("p (j n) -> p j n", j=2))


def out_proj(P, C, wo_ap, wo_sb, kchunks=16):
    wv_ = wo_ap.rearrange("(c p) f -> p c f", p=128)
    bi = 0
    for q in range(4):
        wq = wo_sb[q % len(wo_sb)]
        P.dma("pool", wq[:], wv_[:, :, q * 512:(q + 1) * 512])
        for t in range(C.ntt):
            ps = C.bank[bi % 4]
            bi += 1
            for c in range(kchunks):
                P.matmul(ps[:], C.xT[:, c, t * 128:(t + 1) * 128], wq[:, c, :], start=(c == 0), stop=(c == kchunks - 1))
            hq = C.h[t][:, q * 512:(q + 1) * 512]
            tk = P.tok(C.h[t], q)
            P.tt("dve", hq, ps[:], hq, ALU.add, reads=[ps, tk], writes=[tk])


def final_norm(P, C, w_ap, out_v):
    P.dma("sp", C.wn[:], w_ap.partition_broadcast(128))
    for t in range(C.ntt):
        i = t % 2
        h = C.h[t]
        P.memset("dve", C.ss[i][:], 0.0)
        P.act(C.junk[:], h[:], AF.Square, accum_out=C.ss[i][:])
        P.ts("dve", C.rstd[i][:], C.ss[i][:], 1.0 / D_MODEL, RMS_EPS, ALU.mult, ALU.add)
        P.act(C.rstd[i][:], C.rstd[i][:], AF.Sqrt)
        P.op("dve", lambda e, o=C.rstd[i]: e.reciprocal(o[:], o[:]), reads=[C.rstd[i]], writes=[C.rstd[i]])
        P.stt("dve", h[:], h[:], C.rstd[i][:, 0:1], C.wn[:], ALU.mult, ALU.mult)
        P.dma("sp", out_v[t], h[:])


def build_ffn_prog(ntt=8, G=2, nbuf=3, with_final=False):
    nc = bass.Bass("TRN2", target_bir_lowering=False)
    nt = ntt * 128
    with ExitStack() as stack:
        P = Prog(nc, stack)
        h_in = nc.dram_tensor("h_in", [nt, D_MODEL], F32, kind="ExternalInput").ap()
        wn = nc.dram_tensor("wn", [D_MODEL], F32, kind="ExternalInput").ap()
        wg = nc.dram_tensor("wg", [D_MODEL, D_FF], F32, kind="ExternalInput").ap()
        wu = nc.dram_tensor("wu", [D_MODEL, D_FF], F32, kind="ExternalInput").ap()
        wd = nc.dram_tensor("wd", [D_FF, D_MODEL], F32, kind="ExternalInput").ap()
        if with_final:
            wf = nc.dram_tensor("wf", [D_MODEL], F32, kind="ExternalInput").ap()
        h_out_t = P.dram("h_out", [nt, D_MODEL], F32, kind="ExternalOutput")
        h_out = h_out_t.ap()
        C = Ctx(P, ntt)
        hv = h_in.rearrange("(t p) d -> t p d", p=128)
        ov = h_out.rearrange("(t p) d -> t p d", p=128)
        for t in range(ntt):
            P.dma("sp" if t % 2 == 0 else "act", C.h[t][:], hv[t])
        wbufs = make_wbufs(P, nt, G, nbuf)
        ffn(P, C, wn, wg, wu, wd, wbufs, G=G)
        if with_final:
            final_norm(P, C, wf, ov)
        else:
            for t in range(ntt):
                P.dma("sp", ov[t], C.h[t][:])
        P.finish(P.tok(h_out_t))
        P.emit()
    return nc


def build_qkv_prog(ntt=8, **dbg):
    nc = bass.Bass("TRN2", target_bir_lowering=False)
    nt = ntt * 128
    with ExitStack() as stack:
        P = Prog(nc, stack)
        h_in = nc.dram_tensor("h_in", [nt, D_MODEL], F32, kind="ExternalInput").ap()
        wn = nc.dram_tensor("wn", [D_MODEL], F32, kind="ExternalInput").ap()
        wqkv = nc.dram_tensor("wqkv", [D_MODEL, 3 * D_MODEL], F32, kind="ExternalInput").ap()
        cos_d = nc.dram_tensor("cosT", [128, nt], F32, kind="ExternalInput").ap()
        sin_d = nc.dram_tensor("sinT", [128, nt], F32, kind="ExternalInput").ap()
        rot_d = nc.dram_tensor("rotT", [128, 128], BF16, kind="ExternalInput").ap()
        qT_t = P.dram("qT", [16, 128, nt], BF16, kind="ExternalOutput")
        kT_t = P.dram("kT", [16, 128, nt], BF16, kind="ExternalOutput")
        v_t = P.dram("v", [nt, D_MODEL], BF16, kind="ExternalOutput")
        C = Ctx(P, ntt)
        hv = h_in.rearrange("(t p) d -> t p d", p=128)
        for t in range(ntt):
            P.dma("sp" if t % 2 == 0 else "act", C.h[t][:], hv[t])
        cos_sb = P.sb("cos_sb", [128, nt], F32)
        sin_sb = P.sb("sin_sb", [128, nt], F32)
        rot_sb = P.sb("rot_sb", [128, 128], BF16)
        P.dma("sp", cos_sb[:], cos_d)
        P.dma("sp", sin_sb[:], sin_d)
        P.dma("sp", rot_sb[:], rot_d)
        wbufs = make_wbufs(P, nt, 2, 2, with_ffn=False)
        rmsnorm_to_xT(P, C, wn, C.bank[4:8])
        qkv_rope(P, C, wqkv, cos_sb, sin_sb, rot_sb, qT_t.ap(), kT_t.ap(), v_t.ap(), wbufs, **dbg)
        P.finish(P.tok(qT_t) + P.tok(kT_t) + P.tok(v_t))
        P.emit()
    return nc


def build_att_prog(ntt=8, lambda_init=0.2):
    nc = bass.Bass("TRN2", target_bir_lowering=False)
    nt = ntt * 128
    with ExitStack() as stack:
        P = Prog(nc, stack)
        h_in = nc.dram_tensor("h_in", [nt, D_MODEL], F32, kind="ExternalInput").ap()
        qT_d = nc.dram_tensor("qT", [16, 128, nt], BF16, kind="ExternalInput").ap()
        kTp_d = nc.dram_tensor("kTp", [16, 128, nt], BF16, kind="ExternalInput").ap()
        kTo_d = nc.dram_tensor("kTo", [16, 128, nt], BF16, kind="ExternalInput").ap()
        vp_d = nc.dram_tensor("vp", [nt, D_MODEL], BF16, kind="ExternalInput").ap()
        vo_d = nc.dram_tensor("vo", [nt, D_MODEL], BF16, kind="ExternalInput").ap()
        pb_d = nc.dram_tensor("pbias", [128, 1], F32, kind="ExternalInput").ap()
        lam_d = [nc.dram_tensor(n, [128], F32, kind="ExternalInput").ap() for n in ("lq1", "lk1", "lq2", "lk2")]
        sub_d = nc.dram_tensor("subln", [256], F32, kind="ExternalInput").ap()
        wo_d = nc.dram_tensor("wo", [D_MODEL, D_MODEL], F32, kind="ExternalInput").ap()
        h_out_t = P.dram("h_out", [nt, D_MODEL], F32, kind="ExternalOutput")
        C = Ctx(P, ntt)
        hv = h_in.rearrange("(t p) d -> t p d", p=128)
        ov = h_out_t.ap().rearrange("(t p) d -> t p d", p=128)
        for t in range(ntt):
            P.dma("sp" if t % 2 == 0 else "act", C.h[t][:], hv[t])
        pbias = P.sb("pbias_sb", [128, 1], F32)
        P.dma("sp", pbias[:], pb_d)
        lt = [P.sb(f"lam{i}", [128, 128], F32) for i in range(4)]
        for i in range(4):
            P.dma("sp", lt[i][:], lam_d[i].partition_broadcast(128))
        ls = P.sb("lam_s", [128, 4], F32)
        P.tt("dve", lt[0][:], lt[0][:], lt[1][:], ALU.mult)
        P.tt("dve", lt[2][:], lt[2][:], lt[3][:], ALU.mult)
        P.reduce("dve", ls[:, 0:1], lt[0][:], ALU.add)
        P.reduce("dve", ls[:, 1:2], lt[2][:], ALU.add)
        P.act(ls[:, 0:2], ls[:, 0:2], AF.Exp)
        nlam = P.sb("nlam", [128, 1], F32)
        P.tt("dve", ls[:, 2:3], ls[:, 1:2], ls[:, 0:1], ALU.subtract)
        P.ts("dve", nlam[:], ls[:, 2:3], -float(lambda_init), None, ALU.add)
        sublnw = P.sb("sublnw", [128, 256], F32)
        P.dma("sp", sublnw[:], sub_d.partition_broadcast(128))
        P.ts("dve", sublnw[:], sublnw[:], 1.0 - float(lambda_init), None, ALU.mult)
        attn_core(P, C, qT_d, kTp_d, kTo_d, vp_d, vo_d, pbias, nlam, sublnw, ntt, ntt)
        wo_sb = [P.sb(f"wo{i}", [128, 16, 512], BF16) for i in range(2)]
        out_proj(P, C, wo_d, wo_sb)
        for t in range(ntt):
            P.dma("sp", ov[t], C.h[t][:])
        P.finish(P.tok(h_out_t))
        P.emit()
    return nc


def gdn_inproj(P, C, xTh, w_in, convw_ap, alog_ap, dtb_ap, qT_d, kT_d, vT_d, z_d, beta_d, g_d, wbufs, halo0=125):
    nt = C.nt
    wv_ = w_in.rearrange("(dc p) f -> p dc f", p=128)
    cwj = [P.sb(f"g_cw{j}", [128, 64], F32) for j in range(4)]
    cw_raw = P.sb("g_cwraw", [64, 4, 128], F32)
    P.dma("sp", cw_raw[:], convw_ap.rearrange("j (cb p) -> cb j p", p=128))
    for j in range(4):
        P.transpose(C.bank[j][:, 0:64], cw_raw[:, j, :], C.ident_f[0:64, 0:64])
        P.copy("dve", cwj[j][:], C.bank[j][:, 0:64])
    ones_bf = P.sb("g_ones", [128, 128], BF16)
    P.memset("pool", ones_bf[:], 1.0)
    pre_r = Ring(P, "g_pre", [128, nt + 3], F32, 2)
    acc_r = Ring(P, "g_acc", [128, nt], F32, 2)
    sil_r = Ring(P, "g_sil", [128, nt], F32, 2)
    sq_r = Ring(P, "g_sq", [128, nt], BF16, 2)
    rs_r = Ring(P, "g_rs", [128, nt], F32, 1)
    ob_r = Ring(P, "g_ob", [128, nt], BF16, 2)
    bi = 0
    for cb2 in range(32):
        wb = wbufs[cb2 % len(wbufs)]
        w = wb["wg"] if (cb2 // len(wbufs)) % 2 == 0 else wb["wu"]
        P.dma("pool", w[:], wv_[:, :, cb2 * 256:(cb2 + 1) * 256])
        for sub in range(2):
            cb = cb2 * 2 + sub
            ps0 = C.bank[bi % 2 * 3]
            ps1 = C.bank[bi % 2 * 3 + 1]
            psh = C.bank[bi % 2 * 3 + 2]
            bi += 1
            for dc in range(16):
                lw = w[:, dc, sub * 128:(sub + 1) * 128]
                P.matmul(ps0[:], lw, C.xT[:, dc, 0:512], start=(dc == 0), stop=(dc == 15))
                P.matmul(ps1[:], lw, C.xT[:, dc, 512:1024], start=(dc == 0), stop=(dc == 15))
                P.matmul(psh[:, 0:3], lw, xTh[:, dc, halo0:halo0 + 3], start=(dc == 0), stop=(dc == 15))
            pre = pre_r.next()
            P.copy("act", pre[:, 0:3], psh[:, 0:3])
            P.copy("act", pre[:, 3:515], ps0[:])
            P.copy("act", pre[:, 515:1027], ps1[:])
            acc = acc_r.next()
            P.ts("dve", acc[:], pre[:, 0:nt], cwj[0][:, cb:cb + 1], None, ALU.mult)
            P.stt("dve", acc[:], pre[:, 1:nt + 1], cwj[1][:, cb:cb + 1], acc[:], ALU.mult, ALU.add)
            P.stt("dve", acc[:], pre[:, 2:nt + 2], cwj[2][:, cb:cb + 1], acc[:], ALU.mult, ALU.add)
            P.stt("dve", acc[:], pre[:, 3:nt + 3], cwj[3][:, cb:cb + 1], acc[:], ALU.mult, ALU.add)
            if cb >= 32:
                ob = ob_r.next()
                P.act(ob[:], acc[:], AF.Silu)
                P.dma("sp", vT_d[cb - 32], ob[:])
                continue
            sil = sil_r.next()
            P.act(sil[:], acc[:], AF.Silu)
            sq = sq_r.next()
            P.tt("pool", sq[:], sil[:], sil[:], ALU.mult)
            rs = rs_r.next()
            for th in range(nt // 512):
                pss = C.bank[6 + th % 2]
                P.matmul(pss[:], ones_bf[:], sq[:, th * 512:(th + 1) * 512], start=True, stop=True)
                P.ts("dve", rs[:, th * 512:(th + 1) * 512], pss[:], 1e-6, None, ALU.add)
            P.act(rs[:], rs[:], AF.Sqrt)
            P.op("dve", lambda e, rs=rs: e.reciprocal(rs[:], rs[:]), reads=[rs], writes=[rs])
            ob = ob_r.next()
            P.stt("dve", ob[:], sil[:], (128.0 ** -0.5) if cb < 16 else 1.0, rs[:], ALU.mult, ALU.mult)
            P.dma("sp", (qT_d[cb] if cb < 16 else kT_d[cb - 16]), ob[:])
    zst = Ring(P, "g_zst", [128, 512], BF16, 3)
    for zq in range(8):
        wb = wbufs[zq % len(wbufs)]
        wz = wb["wd"][:].rearrange("p g (a b) -> p (g a) b", b=512) if False else None
        wzt = zw_tiles[zq % 2]
        P.dma("pool", wzt[:], wv_[:, :, 8192 + zq * 512:8192 + (zq + 1) * 512])
        for t in range(C.ntt):
            ps = C.bank[bi % 4]
            bi += 1
            for dc in range(16):
                P.matmul(ps[:], C.xT[:, dc, t * 128:(t + 1) * 128], wzt[:, dc, :], start=(dc == 0), stop=(dc == 15))
            zs = zst.next()
            P.copy("act", zs[:], ps[:])
            P.dma("sp", z_d[t * 128:(t + 1) * 128, zq * 512:(zq + 1) * 512], zs[:])
    wba = P.sb("g_wba", [128, 16, 64], BF16)
    P.dma("pool", wba[:], wv_[:, :, 12288:12352])
    nega = P.sb("g_nega", [128, 32], F32)
    dtb = P.sb("g_dtb", [128, 32], F32)
    P.dma("sp", nega[:], alog_ap.partition_broadcast(128))
    P.dma("sp", dtb[:], dtb_ap.partition_broadcast(128))
    P.act(nega[:], nega[:], AF.Exp)
    P.ts("dve", nega[:], nega[:], -1.0, None, ALU.mult)
    gst = Ring(P, "g_gst", [128, 64], F32, 2)
    gtm = Ring(P, "g_gtm", [128, 32], F32, 2)
    for t in range(C.ntt):
        ps = C.bank[bi % 4]
        bi += 1
        for dc in range(16):
            P.matmul(ps[:, 0:64], C.xT[:, dc, t * 128:(t + 1) * 128], wba[:, dc, :], start=(dc == 0), stop=(dc == 15))
        o = gst.next()
        tm = gtm.next()
        P.tt("dve", tm[:], ps[:, 32:64], dtb[:], ALU.add)
        P.act(o[:, 0:32], ps[:, 0:32], AF.Sigmoid)
        P.act(tm[:], tm[:], AF.Exp)
        P.ts("dve", tm[:], tm[:], 1.0, None, ALU.add)
        P.act(tm[:], tm[:], AF.Ln)
        P.tt("dve", o[:, 32:64], tm[:], nega[:], ALU.mult)
        P.dma("sp", beta_d[t * 128:(t + 1) * 128, :], o[:, 0:32])
        P.dma("sp", g_d[t * 128:(t + 1) * 128, :], o[:, 32:64])


def build_g1_prog(ntt=8):
    nc = bass.Bass("TRN2", target_bir_lowering=False)
    nt = ntt * 128
    with ExitStack() as stack:
        P = Prog(nc, stack)
        h_in = nc.dram_tensor("h_in", [nt, D_MODEL], F32, kind="ExternalInput").ap()
        h_halo = nc.dram_tensor("h_halo", [128, D_MODEL], F32, kind="ExternalInput").ap()
        wn = nc.dram_tensor("wn", [D_MODEL], F32, kind="ExternalInput").ap()
        w_in = nc.dram_tensor("w_in", [D_MODEL, 12352], F32, kind="ExternalInput").ap()
        convw = nc.dram_tensor("conv_w", [4, 8192], F32, kind="ExternalInput").ap()
        alog = nc.dram_tensor("a_log", [32], F32, kind="ExternalInput").ap()
        dtb = nc.dram_tensor("dt_bias", [32], F32, kind="ExternalInput").ap()
        qT_t = P.dram("qT", [16, 128, nt], BF16, kind="ExternalOutput")
        kT_t = P.dram("kT", [16, 128, nt], BF16, kind="ExternalOutput")
        vT_t = P.dram("vT", [32, 128, nt], BF16, kind="ExternalOutput")
        z_t = P.dram("z", [nt, 4096], BF16, kind="ExternalOutput")
        b_t = P.dram("beta", [nt, 32], F32, kind="ExternalOutput")
        g_t = P.dram("g", [nt, 32], F32, kind="ExternalOutput")
        C = Ctx(P, ntt, nh=2)
        hh = P.sb("hh", [128, D_MODEL], F32)
        xTh = P.sb("xTh", [128, 16, 128], BF16)
        hv = h_in.rearrange("(t p) d -> t p d", p=128)
        hs = C.h
        C.h = [hs[t % 2] for t in range(ntt)]
        P.dma("sp", hh[:], h_halo)
        wbufs = make_wbufs(P, nt, 2, 2, with_ffn=False)
        global zw_tiles
        zw_tiles = [P.sb(f"g_wz{i}", [128, 16, 512], BF16) for i in range(2)]
        rmsnorm_stream(P, C, wn, C.bank[4:8], hv, (hh, xTh))
        gdn_inproj(P, C, xTh, w_in, convw, alog, dtb, qT_t.ap(), kT_t.ap(), vT_t.ap(), z_t.ap(), b_t.ap(), g_t.ap(), wbufs)
        P.finish(P.tok(qT_t) + P.tok(kT_t) + P.tok(vT_t) + P.tok(z_t) + P.tok(b_t) + P.tok(g_t))
        P.emit()
    return nc


def rmsnorm_stream(P, C, wnorm_ap, banks, hv, extra):
    P.dma("sp", C.wn[:], wnorm_ap.partition_broadcast(128))
    for t in range(C.ntt + 1):
        i = t % 2
        if t < C.ntt:
            h = C.h[t]
            P.dma("sp" if t % 2 == 0 else "act", h[:], hv[t])
        else:
            h = extra[0]
        P.memset("dve", C.ss[i][:], 0.0)
        P.act(C.junk[:], h[:], AF.Square, accum_out=C.ss[i][:])
        P.ts("dve", C.rstd[i][:], C.ss[i][:], 1.0 / D_MODEL, RMS_EPS, ALU.mult, ALU.add)
        P.act(C.rstd[i][:], C.rstd[i][:], AF.Sqrt)
        P.op("dve", lambda e, o=C.rstd[i]: e.reciprocal(o[:], o[:]), reads=[C.rstd[i]], writes=[C.rstd[i]])
        P.stt("dve", C.xn[i][:], h[:], C.rstd[i][:, 0:1], C.wn[:], ALU.mult, ALU.mult)
        for half in range(2):
            bk = banks[(2 * t + half) % len(banks)]
            tp = bk[:].bitcast(BF16)
            for j in range(8):
                dc = half * 8 + j
                P.transpose(tp[:, j * 128:(j + 1) * 128], C.xn[i][:, dc * 128:(dc + 1) * 128], C.ident[:])
            src = tp.rearrange("p (j n) -> p j n", j=8)
            if t < C.ntt:
                dst = C.xT[:, half * 8:(half + 1) * 8, t * 128:(t + 1) * 128]
            else:
                dst = extra[1][:, half * 8:(half + 1) * 8, :]
            P.copy("act" if half == 0 else "dve", dst, src)


def view(ap, dims):
    return bass.AP(ap.tensor, ap.offset, [list(ap.ap[0])] + [list(d) for d in dims])


def gdn_chunks(P, C, qT_d, kT_d, vT_d, z_d, beta_d, g_d, S0_d, normw_ap, on_d, Sfin_d, nchunks=16, s0_flag=None):
    I64 = C.ident_f[0:64, 0:64]
    ones64 = P.sb("c_ones64", [64, 64], F32)
    neg64 = P.sb("c_neg64", [64, 64], F32)
    onesall = P.sb("c_onesall", [64, 128], F32)
    triu = P.sb("c_triu", [64, 64], F32)
    strict = P.sb("c_strict", [64, 64], F32)
    normw = P.sb("c_normw", [64, 128], F32)
    P.memset("pool", ones64[:], 1.0)
    P.memset("pool", neg64[:], -1.0)
    P.memset("pool", onesall[:], 1.0)
    for (t, base) in ((triu, 0), (strict, -1)):
        P.memset("pool", t[:], 1.0 if t is triu else -1.0)
        P.op("pool", lambda e, t=t, base=base: e.affine_select(out=t[:], in_=t[:], compare_op=ALU.is_ge, fill=0.0,
                                                               base=base, pattern=[[1, 64]], channel_multiplier=-1),
             reads=[t], writes=[t])
    P.dma("sp", normw[:], normw_ap.partition_broadcast(64))
    S = P.sb("S", [128, 32, 128], F32, ntok=32)
    Sb = P.sb("Sb", [128, 32, 128], BF16, ntok=32)
    if S0_d is None:
        P.memset("pool", S[:], 0.0)
    else:
        P.dma("sp", S[:], S0_d.rearrange("h p d -> p h d"))
        if s0_flag is not None:
            P.ts("dve", S[:], S[:], s0_flag[:, 0:1], None, ALU.mult)
    P.copy("pool", Sb[:], S[:])
    kT4 = P.sb("kT4", [128, 16, 256], BF16)
    qT4 = P.sb("qT4", [128, 16, 256], BF16)
    vT4 = P.sb("vT4", [128, 32, 256], BF16)
    onT4 = P.sb("onT4", [128, 32, 256], BF16)
    zc = P.sb("zc", [64, 4096], BF16)
    bc = P.sb("bc", [64, 32], F32)
    gcl = P.sb("gcl", [64, 32], F32)
    gt = P.sb("gates", [64, 5, 32], F32)
    gl = P.sb("gl", [128, 32], F32)
    k_tm = P.sb("k_tm", [64, 16, 128], BF16)
    v_tm = P.sb("v_tm", [64, 32, 128], BF16)
    kgl = P.sb("kgl", [64, 32, 128], BF16)
    tmp = P.sb("tmpA", [64, 4096], F32)
    Dg = tmp[:, 0:2048]
    E = tmp[:, 2048:4096]
    PA = [P.sb(f"PA{i}", [64, 32, 64], F32) for i in range(2)]
    PT = [P.sb(f"PT{i}", [64, 32, 64], F32) for i in range(2)]
    X = P.sb("Xinv", [64, 32, 64], F32)
    Xb = P.sb("Xb", [64, 32, 64], BF16)
    QKd = P.sb("QKd", [64, 32, 64], BF16)
    o_c = P.sb("o_c", [64, 32, 128], F32, ntok=32)
    zs = P.sb("zs", [64, 4096], BF16)
    on_sb = P.sb("on_sb", [64, 4096], BF16)
    nrm = P.sb("nrm", [64, 2, 32], F32)
    r_r = Ring(P, "r_sb", [64, 128], BF16, 4)
    qs_r = Ring(P, "qs_sb", [64, 128], F32, 4)
    vn_r = Ring(P, "vn_sb", [64, 128], BF16, 4)
    bk = C.bank
    for n in range(nchunks):
        co = (n % 4) * 64
        if n % 4 == 0:
            sl = slice(n * 64, n * 64 + 256)
            P.dma("sp", kT4[:], kT_d[:, :, sl].rearrange("k p t -> p k t"))
            P.dma("act", qT4[:], qT_d[:, :, sl].rearrange("k p t -> p k t"))
            P.dma("sp", vT4[:], vT_d[:, :, sl].rearrange("k p t -> p k t"))
        P.dma("act", zc[:], z_d[n * 64:(n + 1) * 64, :])
        P.dma("sp", bc[:], beta_d[n * 64:(n + 1) * 64, :])
        P.dma("sp", gcl[:], g_d[n * 64:(n + 1) * 64, :])
        gp = bk[6]
        P.matmul(gp[0:64, 0:32], triu[:], gcl[:], start=True, stop=True)
        P.matmul(gp[:, 32:64], onesall[:], gcl[:], start=True, stop=True)
        gc = gt[:, 0, :]
        P.copy("dve", gc, gp[0:64, 0:32])
        P.act(gt[:, 1, :], gp[0:64, 0:32], AF.Exp)
        P.act(gt[:, 2, :], gp[0:64, 0:32], AF.Exp)
        P.ts("dve", gt[:, 2, :], gt[:, 2, :], -1.0, None, ALU.mult)
        P.tt("dve", gt[:, 4, :], gp[0:64, 32:64], gc, ALU.subtract)
        P.act(gt[:, 3, :], gt[:, 4, :], AF.Exp)
        P.ts("dve", gt[:, 4, :], gc, -1.0, None, ALU.mult)
        P.act(gl[:], gp[:, 32:64], AF.Exp)
        for blk in range(2):
            tp = bk[7][:].bitcast(BF16)
            for j in range(8):
                kh = blk * 8 + j
                P.transpose(tp[0:64, j * 128:(j + 1) * 128], kT4[:, kh, co:co + 64], C.ident[:])
            P.copy("act", k_tm[:, blk * 8:(blk + 1) * 8, :], tp[0:64, :].rearrange("p (j d) -> p j d", j=8))
        for blk in range(4):
            tp = bk[6 + blk % 2][:].bitcast(BF16)
            for j in range(8):
                hh = blk * 8 + j
                P.transpose(tp[0:64, j * 128:(j + 1) * 128], vT4[:, hh, co:co + 64], C.ident[:])
            P.copy("act" if blk % 2 == 0 else "dve", v_tm[:, blk * 8:(blk + 1) * 8, :],
                   tp[0:64, :].rearrange("p (j d) -> p j d", j=8))
        P.tt("dve", view(kgl[:], [[256, 16], [128, 2], [1, 128]]), view(k_tm[:], [[128, 16], [0, 2], [1, 128]]),
             view(gt[:, 3, :], [[2, 16], [1, 2], [0, 128]]), ALU.mult)
        P.tt("dve", view(Dg, [[64, 32], [1, 64]]), view(I64, [[0, 32], [1, 64]]), view(gc, [[1, 32], [0, 64]]), ALU.mult)
        for gq in range(4):
            dps = bk[6 + gq % 2]
            P.matmul(dps[0:64, :], ones64[:], Dg[:, gq * 512:(gq + 1) * 512], start=True, stop=True)
            Eg = E[:, gq * 512:(gq + 1) * 512]
            for j in range(8):
                h = gq * 8 + j
                P.ts("dve", Eg[:, j * 64:(j + 1) * 64], dps[0:64, j * 64:(j + 1) * 64], gt[:, 4, h:h + 1], 0.0, ALU.add, ALU.min)
            P.act(Eg, Eg, AF.Exp)
            P.tt("pool", view(Eg, [[64, 8], [1, 64]]), view(Eg, [[64, 8], [1, 64]]), view(triu[:], [[0, 8], [1, 64]]), ALU.mult)
        for half in range(2):
            kkp = bk[6]
            qkp = bk[7]
            for j in range(8):
                kh = half * 8 + j
                P.matmul(kkp[0:64, j * 64:(j + 1) * 64], kT4[:, kh, co:co + 64], kT4[:, kh, co:co + 64], start=True, stop=True)
                P.matmul(qkp[0:64, j * 64:(j + 1) * 64], kT4[:, kh, co:co + 64], qT4[:, kh, co:co + 64], start=True, stop=True)
            hs = slice(half * 16, (half + 1) * 16)
            Eh = view(E[:, half * 1024:(half + 1) * 1024], [[128, 8], [64, 2], [1, 64]])
            P.tt("dve", view(QKd[:, hs, :], [[128, 8], [64, 2], [1, 64]]), view(qkp[0:64, :], [[64, 8], [0, 2], [1, 64]]), Eh, ALU.mult)
            P0h = view(PA[0][:, hs, :], [[128, 8], [64, 2], [1, 64]])
            P.tt("dve", P0h, view(kkp[0:64, :], [[64, 8], [0, 2], [1, 64]]), Eh, ALU.mult)
            P.tt("pool", PA[0][:, hs, :], PA[0][:, hs, :], view(bc[:, hs], [[1, 16], [0, 64]]), ALU.mult)
            P.tt("pool", PA[0][:, hs, :], PA[0][:, hs, :], view(strict[:], [[0, 16], [1, 64]]), ALU.mult)
        for gq in range(4):
            tps = bk[6 + gq % 2]
            for j in range(8):
                h = gq * 8 + j
                P.transpose(tps[0:64, j * 64:(j + 1) * 64], PA[0][:, h, :], I64)
            P.copy("act", PT[0][:, gq * 8:(gq + 1) * 8, :], tps[0:64, :].rearrange("p (j c) -> p j c", j=8))
        P.tt("pool", X[:], PA[0][:], view(I64, [[0, 32], [1, 64]]), ALU.add)
        cur = 0
        for lvl in range(5):
            nxt = 1 - cur
            for gq in range(4):
                hs = slice(gq * 8, (gq + 1) * 8)
                if lvl < 4:
                    pp = bk[6]
                    for j in range(8):
                        h = gq * 8 + j
                        P.matmul(pp[0:64, j * 64:(j + 1) * 64], PT[cur][:, h, :], PA[cur][:, h, :], start=True, stop=True)
                    P.copy("act", PA[nxt][:, hs, :], pp[0:64, :].rearrange("p (j c) -> p j c", j=8))
                pt = bk[7]
                for j in range(8):
                    h = gq * 8 + j
                    P.matmul(pt[0:64, j * 64:(j + 1) * 64], PA[cur][:, h, :], PT[cur][:, h, :], start=True, stop=True)
                P.copy("dve", PT[nxt][:, hs, :], pt[0:64, :].rearrange("p (j c) -> p j c", j=8))
                px = bk[6]
                for j in range(8):
                    h = gq * 8 + j
                    P.matmul(px[0:64, j * 64:(j + 1) * 64], PT[nxt][:, h, :], X[:, h, :], start=True, stop=True)
                P.tt("dve", X[:, hs, :], X[:, hs, :], px[0:64, :].rearrange("p (j c) -> p j c", j=8), ALU.add)
            cur = nxt
        P.copy("pool", Xb[:], X[:])
        for h in range(32):
            kh = h // 2
            b0 = (h % 2) * 3
            sps, vps, kvp = bk[b0], bk[b0 + 1], bk[b0 + 2]
            St = P.tok(S, h)
            Sbt = P.tok(Sb, h)
            ot = P.tok(o_c, h)
            P.op("pe", lambda e, sps=sps, kh=kh, h=h, co=co: e.matmul(sps[0:64, 0:128], kT4[:, kh, co:co + 64], Sb[:, h, :], start=True, stop=True),
                 reads=[kT4, Sbt], writes=[sps])
            P.op("pe", lambda e, sps=sps, kh=kh, h=h, co=co: e.matmul(sps[0:64, 128:256], qT4[:, kh, co:co + 64], Sb[:, h, :], start=True, stop=True),
                 reads=[qT4, Sbt], writes=[sps])
            r = r_r.next()
            P.stt("dve", r[:], sps[0:64, 0:128], gt[:, 2, h:h + 1], v_tm[:, h, :], ALU.mult, ALU.add)
            qs = qs_r.next()
            P.act(qs[:], sps[0:64, 128:256], AF.Copy, scale=gt[:, 1, h:h + 1])
            P.matmul(vps[0:64, 0:128], Xb[:, h, :], r[:], start=True, stop=True)
            vn = vn_r.next()
            P.act(vn[:], vps[0:64, 0:128], AF.Copy, scale=bc[:, h:h + 1])
            P.matmul(vps[0:64, 128:256], QKd[:, h, :], vn[:], start=True, stop=True)
            P.tt("dve", o_c[:, h, :], vps[0:64, 128:256], qs[:], ALU.add, reads=[vps, qs], writes=[ot])
            P.matmul(kvp[:, 0:128], kgl[:, h, :], vn[:], start=True, stop=True)
            P.stt("dve", S[:, h, :], S[:, h, :], gl[:, h:h + 1], kvp[:, 0:128], ALU.mult, ALU.add,
                  reads=[St, gl, kvp], writes=[St])
            P.copy("pool", Sb[:, h, :], S[:, h, :], reads=[St], writes=[Sbt])
        P.tt("pool", tmp[:], o_c[:].rearrange("p h d -> p (h d)"), o_c[:].rearrange("p h d -> p (h d)"), ALU.mult)
        P.reduce("dve", nrm[:, 0, :], tmp[:].rearrange("p (h d) -> p h d", h=32), ALU.add)
        P.ts("dve", nrm[:, 1, :], nrm[:, 0, :], 1.0 / 128.0, RMS_EPS, ALU.mult, ALU.add)
        P.act(nrm[:, 1, :], nrm[:, 1, :], AF.Sqrt)
        P.op("dve", lambda e: e.reciprocal(nrm[:, 1, :], nrm[:, 1, :]), reads=[nrm], writes=[nrm])
        P.act(zs[:], zc[:], AF.Silu)
        P.tt("dve", o_c[:], o_c[:], view(nrm[:, 1, :], [[1, 32], [0, 128]]), ALU.mult)
        P.tt("pool", o_c[:], o_c[:], view(normw[:], [[0, 32], [1, 128]]), ALU.mult)
        P.tt("dve", on_sb[:], o_c[:].rearrange("p h d -> p (h d)"), zs[:], ALU.mult)
        for blk in range(2):
            tp = bk[6 + blk][:].bitcast(BF16)
            for j in range(16):
                h = blk * 16 + j
                P.transpose(tp[:, j * 64:(j + 1) * 64], on_sb[:, h * 128:(h + 1) * 128], C.ident[0:64, 0:64])
            P.copy("act", onT4[:, blk * 16:(blk + 1) * 16, co:co + 64], tp.rearrange("p (j t) -> p j t", j=16))
        if n % 4 == 3:
            sl = slice((n - 3) * 64, (n + 1) * 64)
            P.dma("sp", on_d[:, :, sl].rearrange("k p t -> p k t"), onT4[:])
    P.dma("sp", Sfin_d.rearrange("h p d -> p h d"), S[:])


def build_g2_prog(nchunks=16):
    nc = bass.Bass("TRN2", target_bir_lowering=False)
    nt = nchunks * 64
    with ExitStack() as stack:
        P = Prog(nc, stack)
        qT_d = nc.dram_tensor("qT", [16, 128, nt], BF16, kind="ExternalInput").ap()
        kT_d = nc.dram_tensor("kT", [16, 128, nt], BF16, kind="ExternalInput").ap()
        vT_d = nc.dram_tensor("vT", [32, 128, nt], BF16, kind="ExternalInput").ap()
        z_d = nc.dram_tensor("z", [nt, 4096], BF16, kind="ExternalInput").ap()
        b_d = nc.dram_tensor("beta", [nt, 32], F32, kind="ExternalInput").ap()
        g_d = nc.dram_tensor("g", [nt, 32], F32, kind="ExternalInput").ap()
        S0_d = nc.dram_tensor("S0", [32, 128, 128], F32, kind="ExternalInput").ap()
        nw_d = nc.dram_tensor("normw", [128], F32, kind="ExternalInput").ap()
        on_t = P.dram("onT", [32, 128, nt], BF16, kind="ExternalOutput")
        Sf_t = P.dram("Sfin", [32, 128, 128], F32, kind="ExternalOutput")
        C = Ctx(P, nt // 128, light=True)
        gdn_chunks(P, C, qT_d, kT_d, vT_d, z_d, b_d, g_d, S0_d, nw_d, on_t.ap(), Sf_t.ap(), nchunks=nchunks)
        P.finish(P.tok(on_t) + P.tok(Sf_t))
        P.emit()
    return nc


def rope_consts(half, nt=1024):
    inv = 1.0 / (10000.0 ** (np.arange(0, 128, 2, dtype=np.float32) / 128.0))
    pos = np.arange(half * nt, (half + 1) * nt, dtype=np.float32)
    ang = pos[None, :] * np.concatenate([inv, inv])[:, None].astype(np.float32)
    return np.cos(ang).astype(np.float32), np.sin(ang).astype(np.float32)


def rot_const():
    import ml_dtypes
    r = np.zeros((128, 128), np.float32)
    for m in range(64):
        r[m + 64, m] = -1.0
        r[m, m + 64] = 1.0
    return r.astype(ml_dtypes.bfloat16)


def build_g3_prog(ntt=8):
    nc = bass.Bass("TRN2", target_bir_lowering=False)
    nt = ntt * 128
    with ExitStack() as stack:
        P = Prog(nc, stack)
        h_in = nc.dram_tensor("h_in", [nt, D_MODEL], F32, kind="ExternalInput").ap()
        on_d = nc.dram_tensor("onT", [32, 128, nt], BF16, kind="ExternalInput").ap()
        wo_d = nc.dram_tensor("wo", [4096, D_MODEL], F32, kind="ExternalInput").ap()
        h_out_t = P.dram("h_out", [nt, D_MODEL], F32, kind="ExternalOutput")
        C = Ctx(P, ntt, light=True)
        C.h = [P.sb(f"h{t}", [128, D_MODEL], F32, ntok=4) for t in range(ntt)]
        C.xT = P.sb("xT", [128, 32, nt], BF16)
        hv = h_in.rearrange("(t p) d -> t p d", p=128)
        ov = h_out_t.ap().rearrange("(t p) d -> t p d", p=128)
        for t in range(ntt):
            P.dma("sp" if t % 2 == 0 else "act", C.h[t][:], hv[t])
        for k4 in range(4):
            P.dma("sp" if k4 % 2 == 0 else "act", C.xT[:, k4 * 8:(k4 + 1) * 8, :],
                  on_d[k4 * 8:(k4 + 1) * 8].rearrange("k p t -> p k t"))
        wo_sb = [P.sb(f"wo{i}", [128, 32, 512], BF16) for i in range(2)]
        out_proj(P, C, wo_d, wo_sb, kchunks=32)
        for t in range(ntt):
            P.dma("sp", ov[t], C.h[t][:])
        P.finish(P.tok(h_out_t))
        P.emit()
    return nc


def build_fused_prog():
    nc = bass.Bass("TRN2", target_bir_lowering=False)
    ntt, nt = 8, 1024
    groups = [[0, 1], [2, 3], [4, 5], [6, 7]]
    ext = lambda n, sh, dt=F32: nc.dram_tensor(n, list(sh), dt, kind="ExternalInput").ap()
    with ExitStack() as outer:
        P = Prog(nc, outer)
        x_d = ext("x", [nt, D_MODEL])
        f1n, f1g, f1u, f1d = ext("f1n", [2, D_MODEL]), ext("f1g", [2, D_MODEL, D_FF]), ext("f1u", [2, D_MODEL, D_FF]), ext("f1d", [2, D_FF, D_MODEL])
        f2n, f2g, f2u, f2d = ext("f2n", [2, D_MODEL]), ext("f2g", [2, D_MODEL, D_FF]), ext("f2u", [2, D_MODEL, D_FF]), ext("f2d", [2, D_FF, D_MODEL])
        mixn = ext("mixn", [2, D_MODEL])
        wqkv = ext("wqkv", [D_MODEL, 3 * D_MODEL])
        lam_d = [ext(n, [128]) for n in ("lq1", "lk1", "lq2", "lk2")]
        sub_d = ext("subln", [256])
        wo_d = ext("da_wo", [D_MODEL, D_MODEL])
        w_in = ext("w_in", [D_MODEL, 12352])
        convw = ext("conv_w", [4, 8192])
        alog, dtb, gnorm = ext("a_log", [32]), ext("dt_bias", [32]), ext("gnorm", [128])
        gwo_d = ext("g_wo", [4096, D_MODEL])
        fin_d = ext("fin", [D_MODEL])
        cos_d, sin_d = ext("cosT", [128, nt]), ext("sinT", [128, nt])
        rot_d = ext("rotT", [128, 128], BF16)
        pb_d, fl_d = ext("pbias", [128, 1]), ext("pflag", [128, 1])
        out_t = P.dram("out", [nt, D_MODEL], F32, kind="ExternalOutput")
        qT_t = P.dram("s_qT", [2048, nt], BF16)
        kT_t = [P.dram(f"s_kT{i}", [1024, nt], BF16) for i in range(2)]
        v_t = [P.dram(f"s_v{i}", [nt // 2, D_MODEL], BF16) for i in range(2)]
        kTp_t = [P.dram(f"s_kTpair{i}", [2048, nt], BF16) for i in range(2)]
        vp_t = [P.dram(f"s_vpair{i}", [nt, D_MODEL], BF16) for i in range(2)]
        hsp_t = P.dram("s_hspill", [nt, D_MODEL], F32)
        hal_t = P.dram("s_halo", [8, D_MODEL], F32)
        halp_t = P.dram("s_halopair", [16, D_MODEL], F32)
        gq_t = P.dram("s_gq", [2048, nt], BF16)
        gk_t = P.dram("s_gk", [2048, nt], BF16)
        gv_t = P.dram("s_gv", [4096, nt], BF16)
        gz_t = P.dram("s_gz", [nt, 4096], BF16)
        gb_t = P.dram("s_gb", [nt, 32], F32)
        gg_t = P.dram("s_gg", [nt, 32], F32)
        on_t = P.dram("s_on", [4096, nt], BF16)
        on2_t = P.dram("s_on2", [4096, nt], BF16)
        sf_t = P.dram("s_sf", [4096, 128], F32)
        sf2_t = P.dram("s_sf2", [4096, 128], F32)
        sp_t = P.dram("s_spair", [8192, 128], F32)
        v3 = lambda t, k: t.ap().rearrange("(s p) t -> s p t", p=128)
        C = Ctx(P, ntt, light=True)
        P.phase = 1
        hv = x_d.rearrange("(t p) d -> t p d", p=128)
        ov = out_t.ap().rearrange("(t p) d -> t p d", p=128)
        spv = hsp_t.ap().rearrange("(t p) d -> t p d", p=128)

        def phase(fn):
            with ExitStack() as ps:
                P.stack = ps
                fn()
                P.end_phase()
            P.stack = outer

        with ExitStack() as hstack:
            P.stack = hstack
            C.h = [P.sb(f"h{t}", [128, D_MODEL], F32, ntok=4) for t in range(ntt)]
            for t in range(ntt):
                P.dma("sp" if t % 2 == 0 else "act", C.h[t][:], hv[t])

            def ph_ffn(wn, wg, wu, wd):
                def f():
                    alloc_norm(P, C)
                    ffn(P, C, wn, wg, wu, wd, make_wbufs(P, nt, 2, 2), G=2)
                return f

            phase(ph_ffn(f1n[0], f1g[0], f1u[0], f1d[0]))

            def ph_qkv():
                alloc_norm(P, C)
                cos_sb = P.sb("cos_sb", [128, nt], F32)
                sin_sb = P.sb("sin_sb", [128, nt], F32)
                rot_sb = P.sb("rot_sb", [128, 128], BF16)
                P.dma("sp", cos_sb[:], cos_d)
                P.dma("sp", sin_sb[:], sin_d)
                P.dma("sp", rot_sb[:], rot_d)
                wb = make_wbufs(P, nt, 2, 2, with_ffn=False)
                rmsnorm_to_xT(P, C, mixn[0], C.bank[4:8])
                kown = lambda hd: v3(kT_t[hd // 4], 8)[2 * (hd % 4):2 * (hd % 4) + 2]
                qkv_rope(P, C, wqkv, cos_sb, sin_sb, rot_sb, v3(qT_t, 16), kown, [v_t[0].ap(), v_t[1].ap()], wb)
                for i in range(2):
                    P.collective("AllGather", kT_t[i].ap(), kTp_t[i].ap(), groups)
                    P.collective("AllGather", v_t[i].ap(), vp_t[i].ap(), groups)

            phase(ph_qkv)

            def ph_att():
                alloc_norm(P, C, xT=True)
                pbias = P.sb("pbias_sb", [128, 1], F32)
                P.dma("sp", pbias[:], pb_d)
                lt = [P.sb(f"lam{i}", [128, 128], F32) for i in range(4)]
                for i in range(4):
                    P.dma("sp", lt[i][:], lam_d[i].partition_broadcast(128))
                ls = P.sb("lam_s", [128, 4], F32)
                li = 0.8 - 0.6 * math.exp(-0.3 * 0)
                P.tt("dve", lt[0][:], lt[0][:], lt[1][:], ALU.mult)
                P.tt("dve", lt[2][:], lt[2][:], lt[3][:], ALU.mult)
                P.reduce("dve", ls[:, 0:1], lt[0][:], ALU.add)
                P.reduce("dve", ls[:, 1:2], lt[2][:], ALU.add)
                P.act(ls[:, 0:2], ls[:, 0:2], AF.Exp)
                nlam = P.sb("nlam", [128, 1], F32)
                P.tt("dve", ls[:, 2:3], ls[:, 1:2], ls[:, 0:1], ALU.subtract)
                P.ts("dve", nlam[:], ls[:, 2:3], -float(li), None, ALU.add)
                sublnw = P.sb("sublnw", [128, 256], F32)
                P.dma("sp", sublnw[:], sub_d.partition_broadcast(128))
                P.ts("dve", sublnw[:], sublnw[:], 1.0 - float(li), None, ALU.mult)
                kown = lambda hd: v3(kT_t[hd // 4], 8)[2 * (hd % 4):2 * (hd % 4) + 2]
                kprev = lambda hd: kTp_t[hd // 4].ap()[0:1024, :].rearrange("(s p) t -> s p t", p=128)[2 * (hd % 4):2 * (hd % 4) + 2]
                vprev = [vp_t[i].ap()[0:nt // 2, :] for i in range(2)]
                attn_core(P, C, v3(qT_t, 16), kprev, kown, vprev, [v_t[0].ap(), v_t[1].ap()], pbias, nlam, sublnw, ntt, ntt)
                wo_sb = [P.sb(f"wo{i}", [128, 16, 512], BF16) for i in range(2)]
                out_proj(P, C, wo_d, wo_sb)

            phase(ph_att)
            phase(ph_ffn(f2n[0], f2g[0], f2u[0], f2d[0]))
            phase(ph_ffn(f1n[1], f1g[1], f1u[1], f1d[1]))

            def ph_spill():
                for t in range(ntt):
                    P.dma("sp" if t % 2 == 0 else "act", spv[t], C.h[t][:])
                P.dma("sp", hal_t.ap()[0:3, :], C.h[ntt - 1][125:128, :])
                P.collective("AllGather", hal_t.ap(), halp_t.ap(), groups)

            phase(ph_spill)
            P.stack = outer
        def ph_g1():
            C.h = [P.sb(f"hs{i}", [128, D_MODEL], F32, ntok=4) for i in range(2)]
            C.h = [C.h[t % 2] for t in range(ntt)]
            alloc_norm(P, C)
            hh = P.sb("hh", [128, D_MODEL], F32)
            xTh = P.sb("xTh", [128, 16, 128], BF16)
            flag = P.sb("flag_sb", [128, 1], F32)
            P.dma("sp", flag[:], fl_d)
            P.memset("pool", hh[:], 0.0)
            P.dma("sp", hh[0:3, :], halp_t.ap()[0:3, :])
            P.ts("dve", hh[:], hh[:], flag[:, 0:1], None, ALU.mult)
            wb = make_wbufs(P, nt, 2, 2, with_ffn=False)
            global zw_tiles
            zw_tiles = [P.sb(f"g_wz{i}", [128, 16, 512], BF16) for i in range(2)]
            rmsnorm_stream(P, C, mixn[1], C.bank[4:8], spv, (hh, xTh))
            gdn_inproj(P, C, xTh, w_in, convw, alog, dtb, v3(gq_t, 16), v3(gk_t, 16), v3(gv_t, 32), gz_t.ap(), gb_t.ap(), gg_t.ap(),
                       wb, halo0=0)

        phase(ph_g1)

        def ph_g2a():
            gdn_chunks(P, C, v3(gq_t, 16), v3(gk_t, 16), v3(gv_t, 32), gz_t.ap(), gb_t.ap(), gg_t.ap(), None, gnorm,
                       v3(on_t, 32), sf_t.ap().rearrange("(h p) d -> h p d", p=128))
            P.collective("AllGather", sf_t.ap(), sp_t.ap(), groups)

        phase(ph_g2a)

        def ph_g2b():
            flag = P.sb("flag_sb", [128, 1], F32)
            P.dma("sp", flag[:], fl_d)
            gdn_chunks(P, C, v3(gq_t, 16), v3(gk_t, 16), v3(gv_t, 32), gz_t.ap(), gb_t.ap(), gg_t.ap(),
                       sp_t.ap()[0:4096, :].rearrange("(h p) d -> h p d", p=128), gnorm,
                       v3(on2_t, 32), sf2_t.ap().rearrange("(h p) d -> h p d", p=128), s0_flag=flag)

        phase(ph_g2b)
        with ExitStack() as hstack:
            P.stack = hstack
            C.h = [P.sb(f"h{t}", [128, D_MODEL], F32, ntok=4) for t in range(ntt)]
            for t in range(ntt):
                P.dma("sp" if t % 2 == 0 else "act", C.h[t][:], spv[t])

            def ph_g3():
                C.xT = P.sb("xT", [128, 32, nt], BF16)
                on3 = v3(on2_t, 32)
                for k4 in range(4):
                    P.dma("sp" if k4 % 2 == 0 else "act", C.xT[:, k4 * 8:(k4 + 1) * 8, :],
                          on3[k4 * 8:(k4 + 1) * 8].rearrange("k p t -> p k t"))
                wo_sb = [P.sb(f"wo{i}", [128, 32, 512], BF16) for i in range(2)]
                out_proj(P, C, gwo_d, wo_sb, kchunks=32)

            phase(ph_g3)

            def ph_last():
                alloc_norm(P, C)
                ffn(P, C, f2n[1], f2g[1], f2u[1], f2d[1], make_wbufs(P, nt, 2, 2), G=2)
                final_norm(P, C, fin_d, ov)
                P.finish(P.tok(out_t))

            phase(ph_last)
            P.stack = outer
    return nc


_PROGS = {}
_DEBUG = None


def _prog(name, fn):
    if name not in _PROGS:
        _PROGS[name] = fn()
    return _PROGS[name]


def _launch(name, nc, in_maps):
    res = run_bass_kernel_spmd(nc, in_maps, core_ids=list(range(8)))
    if _DEBUG is not None:
        _DEBUG[name] = res.results
    return res.results


def _c(a):
    return np.ascontiguousarray(a)


def kernel(x, ffn1_norm, ffn1_w_gate, ffn1_w_up, ffn1_w_down, mix_norm,
           ffn2_norm, ffn2_w_gate, ffn2_w_up, ffn2_w_down,
           da_w_qkv, da_lambda_q1, da_lambda_k1, da_lambda_q2, da_lambda_k2, da_subln, da_w_o,
           gdn_w_in, gdn_conv_w, gdn_a_log, gdn_dt_bias, gdn_norm, gdn_w_o, final_norm):
    f32 = np.float32
    nt = 1024
    a = lambda v: _c(np.asarray(v, f32))
    nc = _prog("fused", build_fused_prog)
    xs = a(x).reshape(-1, D_MODEL)
    shared = {
        "f1n": a(ffn1_norm), "f1g": a(ffn1_w_gate), "f1u": a(ffn1_w_up), "f1d": a(ffn1_w_down),
        "f2n": a(ffn2_norm), "f2g": a(ffn2_w_gate), "f2u": a(ffn2_w_up), "f2d": a(ffn2_w_down),
        "mixn": a(mix_norm), "wqkv": a(da_w_qkv[0]),
        "lq1": a(da_lambda_q1[0]), "lk1": a(da_lambda_k1[0]), "lq2": a(da_lambda_q2[0]), "lk2": a(da_lambda_k2[0]),
        "subln": a(da_subln[0]), "da_wo": a(da_w_o[0]),
        "w_in": a(gdn_w_in[0]), "conv_w": a(gdn_conv_w[0]), "a_log": a(gdn_a_log[0]), "dt_bias": a(gdn_dt_bias[0]),
        "gnorm": a(gdn_norm[0]), "g_wo": a(gdn_w_o[0]), "fin": a(final_norm), "rotT": rot_const(),
    }
    maps = []
    for c in range(8):
        odd = c % 2 == 1
        cs, sn = rope_consts(c % 2)
        m = dict(shared)
        m.update({"x": _c(xs[c * nt:(c + 1) * nt]), "cosT": cs, "sinT": sn,
                  "pbias": np.full((128, 1), 0.0 if odd else -30000.0, f32),
                  "pflag": np.full((128, 1), 1.0 if odd else 0.0, f32)})
        maps.append(m)
    r = _launch("fused", nc, maps)
    return np.concatenate([r[c]["out"] for c in range(8)], axis=0).reshape(BATCH, SEQ, D_MODEL).astype(f32)


def kernel_unfused(x, ffn1_norm, ffn1_w_gate, ffn1_w_up, ffn1_w_down, mix_norm,
           ffn2_norm, ffn2_w_gate, ffn2_w_up, ffn2_w_down,
           da_w_qkv, da_lambda_q1, da_lambda_k1, da_lambda_q2, da_lambda_k2, da_subln, da_w_o,
           gdn_w_in, gdn_conv_w, gdn_a_log, gdn_dt_bias, gdn_norm, gdn_w_o, final_norm):
    f32 = np.float32
    nt = 1024
    hs = [_c(np.asarray(x, f32).reshape(-1, D_MODEL)[c * nt:(c + 1) * nt]) for c in range(8)]

    def run_ffn(hs, wn, wg, wu, wd, wf=None):
        if wf is None:
            nc = _prog("ffn", lambda: build_ffn_prog(8, 2, 2, False))
        else:
            nc = _prog("ffn_final", lambda: build_ffn_prog(8, 2, 2, True))
        wn, wg, wu, wd = (_c(np.asarray(a, f32)) for a in (wn, wg, wu, wd))
        maps = []
        for c in range(8):
            m = {"h_in": hs[c], "wn": wn, "wg": wg, "wu": wu, "wd": wd}
            if wf is not None:
                m["wf"] = _c(np.asarray(wf, f32))
            maps.append(m)
        r = _launch("ffn", nc, maps)
        return [r[c]["h_out"] for c in range(8)]

    hs = run_ffn(hs, ffn1_norm[0], ffn1_w_gate[0], ffn1_w_up[0], ffn1_w_down[0])
    nc = _prog("qkv", build_qkv_prog)
    rot = rot_const()
    maps = []
    for c in range(8):
        cs, sn = rope_consts(c % 2)
        maps.append({"h_in": hs[c], "wn": _c(np.asarray(mix_norm[0], f32)), "wqkv": _c(np.asarray(da_w_qkv[0], f32)),
                     "cosT": cs, "sinT": sn, "rotT": rot})
    r = _launch("qkv", nc, maps)
    nc = _prog("att", lambda: build_att_prog(8, 0.8 - 0.6 * math.exp(-0.3 * 0)))
    maps = []
    for c in range(8):
        odd = c % 2 == 1
        maps.append({
            "h_in": hs[c], "qT": r[c]["qT"],
            "kTp": r[c - 1]["kT"] if odd else np.zeros_like(r[c]["kT"]),
            "kTo": r[c]["kT"],
            "vp": r[c - 1]["v"] if odd else np.zeros_like(r[c]["v"]),
            "vo": r[c]["v"],
            "pbias": np.full((128, 1), 0.0 if odd else -30000.0, f32),
            "lq1": _c(np.asarray(da_lambda_q1[0], f32)), "lk1": _c(np.asarray(da_lambda_k1[0], f32)),
            "lq2": _c(np.asarray(da_lambda_q2[0], f32)), "lk2": _c(np.asarray(da_lambda_k2[0], f32)),
            "subln": _c(np.asarray(da_subln[0], f32)), "wo": _c(np.asarray(da_w_o[0], f32)),
        })
    r = _launch("att", nc, maps)
    hs = [r[c]["h_out"] for c in range(8)]
    hs = run_ffn(hs, ffn2_norm[0], ffn2_w_gate[0], ffn2_w_up[0], ffn2_w_down[0])
    hs = run_ffn(hs, ffn1_norm[1], ffn1_w_gate[1], ffn1_w_up[1], ffn1_w_down[1])
    nc = _prog("g1", build_g1_prog)
    maps = []
    for c in range(8):
        halo = np.zeros((128, D_MODEL), f32)
        if c % 2 == 1:
            halo[125:128] = hs[c - 1][nt - 3:nt]
        maps.append({"h_in": hs[c], "h_halo": halo, "wn": _c(np.asarray(mix_norm[1], f32)),
                     "w_in": _c(np.asarray(gdn_w_in[0], f32)), "conv_w": _c(np.asarray(gdn_conv_w[0], f32)),
                     "a_log": _c(np.asarray(gdn_a_log[0], f32)), "dt_bias": _c(np.asarray(gdn_dt_bias[0], f32))})
    g1 = _launch("g1", nc, maps)
    nc = _prog("g2", build_g2_prog)
    nw = _c(np.asarray(gdn_norm[0], f32))

    def g2_maps(S0s):
        return [{"qT": g1[c]["qT"], "kT": g1[c]["kT"], "vT": g1[c]["vT"], "z": g1[c]["z"], "beta": g1[c]["beta"],
                 "g": g1[c]["g"], "S0": S0s[c], "normw": nw} for c in range(8)]

    zero_S = np.zeros((32, 128, 128), f32)
    ra = _launch("g2a", nc, g2_maps([zero_S] * 8))
    rb = _launch("g2b", nc, g2_maps([ra[c - 1]["Sfin"] if c % 2 == 1 else zero_S for c in range(8)]))
    onT = [rb[c]["onT"] if c % 2 == 1 else ra[c]["onT"] for c in range(8)]
    nc = _prog("g3", build_g3_prog)
    r = _launch("g3", nc, [{"h_in": hs[c], "onT": onT[c], "wo": _c(np.asarray(gdn_w_o[0], f32))} for c in range(8)])
    hs = [r[c]["h_out"] for c in range(8)]
    hs = run_ffn(hs, ffn2_norm[1], ffn2_w_gate[1], ffn2_w_up[1], ffn2_w_down[1], wf=final_norm)
    out = np.concatenate(hs, axis=0).reshape(BATCH, SEQ, D_MODEL).astype(f32)
    return out
```

```python
import math
from contextlib import ExitStack

import numpy as np

import concourse.bass as bass
import concourse.mybir as mybir
from concourse.bass_utils import run_bass_kernel_spmd

F32 = mybir.dt.float32
BF16 = mybir.dt.bfloat16
AF = mybir.ActivationFunctionType
ALU = mybir.AluOpType
AX = mybir.AxisListType

ENGS = ("pe", "act", "dve", "pool", "sp")
SEM_EPOCH = 30000

D_MODEL = 2048
D_FF = 5632
SEQ = 2048
BATCH = 4
RMS_EPS = 1e-6


class Tok:
    def __init__(self, name, space="sb"):
        self.name = name
        self.space = space
        self.writers = {}
        self.dma_writers = []
        self.readers = {}
        self.dma_readers = []
        self.group_deps = []
        self.reading = True
        self.dma_sem = None
        self.dma_cnt = 0


class Op:
    __slots__ = ("eng", "idx", "emit", "is_dma", "dma_sem", "dma_val", "waits", "dwaits",
                 "clock", "dknow", "signal", "sigval", "inc")

    def __init__(self, eng, emit, is_dma):
        self.eng = eng
        self.emit = emit
        self.is_dma = is_dma
        self.idx = -1
        self.dma_sem = None
        self.dma_val = 0
        self.waits = []
        self.dwaits = []
        self.clock = None
        self.dknow = None
        self.signal = False
        self.sigval = None
        self.inc = 16


class Prog:
    def __init__(self, nc, stack):
        self.nc = nc
        self.stack = stack
        self.streams = {e: [] for e in ENGS}
        self.nidx = {e: 0 for e in ENGS}
        self.clock = {e: {} for e in ENGS}
        self.dknow = {e: {} for e in ENGS}
        self.toks = {}
        self.nsem = 0
        self.out_dmas = []
        self.outer = stack
        self.phase = 0
        self.phase_toks = []
        self.free_sems = []
        self.all_sems = {}
        self.esems = {e: [] for e in ENGS}
        self.sigcount = {e: 0 for e in ENGS}

    def sem(self, name):
        self.nsem += 1
        return self.outer.enter_context(self.nc.semaphore(name))

    def sb(self, name, shape, dtype, ntok=1):
        if self.phase:
            name = f"{name}_p{self.phase}"
        h = self.stack.enter_context(self.nc.sbuf_tensor(name, list(shape), dtype))
        self.toks[h.name] = [Tok(f"{name}.{i}") for i in range(ntok)]
        if self.stack is not self.outer:
            self.phase_toks.extend(self.toks[h.name])
        return h

    def ps(self, name, shape, dtype=F32):
        h = self.outer.enter_context(self.nc.psum_tensor(name, list(shape), dtype))
        self.toks[h.name] = [Tok(name, "ps")]
        return h

    def dram(self, name, shape, dtype, kind="Internal", track=True):
        h = self.nc.dram_tensor(name, list(shape), dtype, kind=kind)
        if track:
            self.toks[h.name] = [Tok(name, "dram")]
        return h

    def tok(self, h, i=None):
        t = self.toks[h.name]
        return t if i is None else [t[i]]

    def _toks_of(self, aps):
        out = []
        for a in aps:
            if a is None or isinstance(a, (int, float)):
                continue
            if isinstance(a, Tok):
                out.append(a)
                continue
            if isinstance(a, (list, tuple)):
                out.extend(self._toks_of(a))
                continue
            nm = a.tensor.name if hasattr(a, "tensor") else a.name
            if nm in self.toks:
                out.extend(self.toks[nm])
        seen = set()
        res = []
        for t in out:
            if id(t) not in seen:
                seen.add(id(t))
                res.append(t)
        return res

    def op(self, eng, emit, reads=(), writes=(), is_dma=False, ordered=False, inc=16):
        X = Op(eng, emit, is_dma)
        R = self._toks_of(reads)
        W = self._toks_of(writes)
        deps = []
        for t in R:
            for d in t.writers.values():
                deps.append((d, "RAW"))
            for d in t.dma_writers:
                deps.append((d, "RAW"))
        for t in R:
            if t.space == "ps":
                for e2, d in t.readers.items():
                    if e2 != eng:
                        deps.append((d, "RAR"))
        for t in W:
            other = (any(e != eng for e in t.writers) or bool(t.dma_writers)) if not is_dma else bool(t.writers)
            if t.reading or ordered or other:
                gd = [(d, "WAW") for d in t.writers.values()] + [(d, "WAW") for d in t.dma_writers]
                gd += [(d, "WAR") for d in t.readers.values()] + [(d, "WAR") for d in t.dma_readers]
                t.group_deps = gd
                t.writers = {}
                t.dma_writers = []
                t.readers = {}
                t.dma_readers = []
                t.reading = False
            deps.extend(t.group_deps)
        wset = set(id(t) for t in W)
        for t in R:
            if id(t) in wset:
                continue
            t.reading = True
            if is_dma:
                t.dma_readers.append(X)
            else:
                t.readers[eng] = X
        for t in W:
            if is_dma:
                t.dma_writers.append(X)
            else:
                t.writers[eng] = X
        for t in R:
            if id(t) in wset:
                t.reading = True

        clock = self.clock[eng]
        dknow = self.dknow[eng]
        dmax = {}
        for (D, kind) in deps:
            if D.is_dma and D is not X:
                if dmax.get(D.dma_sem, 0) < D.dma_val:
                    dmax[D.dma_sem] = D.dma_val
        for (D, kind) in deps:
            if D is X:
                continue
            if D.is_dma:
                if D.dma_val < dmax[D.dma_sem]:
                    continue
                if dknow.get(D.dma_sem, 0) >= D.dma_val:
                    continue
                X.dwaits.append((D.dma_sem, D.dma_val))
                dknow[D.dma_sem] = D.dma_val
            else:
                if D.eng == eng:
                    if eng == "pe" or kind == "WAR":
                        continue
                if clock.get(D.eng, -1) >= D.idx:
                    continue
                X.waits.append(D)
                D.signal = True
                clock[D.eng] = D.idx
            for e, i in D.clock.items():
                if clock.get(e, -1) < i:
                    clock[e] = i
            for s, v in D.dknow.items():
                if dknow.get(s, 0) < v:
                    dknow[s] = v
        X.clock = dict(clock)
        X.dknow = dict(dknow)
        if is_dma:
            cand = [t for t in W if t.space == "sb"] or [t for t in R if t.space == "sb"] or (W + R)
            owner = cand[0]
            if owner.dma_sem is None:
                if self.free_sems:
                    owner.dma_sem, owner.dma_cnt = self.free_sems.pop()
                else:
                    owner.dma_sem = self.sem("d_" + owner.name.replace(".", "_"))
            owner.dma_cnt += inc
            X.dma_sem = owner.dma_sem
            X.dma_val = owner.dma_cnt
            X.inc = inc
            self.all_sems[owner.dma_sem] = owner.dma_cnt
        else:
            X.idx = self.nidx[eng]
            self.nidx[eng] += 1
        self.streams[eng].append(X)
        return X

    def matmul(self, out, lhsT, rhs, start=True, stop=True, **kw):
        return self.op("pe", lambda e: e.matmul(out, lhsT, rhs, start=start, stop=stop, **kw),
                       reads=[lhsT, rhs], writes=[out])

    def transpose(self, out, in_, ident):
        return self.op("pe", lambda e: e.transpose(out, in_, ident), reads=[in_, ident], writes=[out])

    def act(self, out, in_, func, bias=None, scale=1.0, accum_out=None, reads=None, writes=None):
        kw = {}
        if bias is not None:
            kw["bias"] = bias
        if accum_out is not None:
            kw["accum_out"] = accum_out
        r = [in_, bias, scale] if reads is None else reads
        w = [out, accum_out] if writes is None else writes
        return self.op("act", lambda e: e.activation(out, in_, func, scale=scale, **kw), reads=r, writes=w)

    def tt(self, eng, out, in0, in1, op, reads=None, writes=None):
        r = [in0, in1] if reads is None else reads
        w = [out] if writes is None else writes
        return self.op(eng, lambda e: e.tensor_tensor(out, in0, in1, op), reads=r, writes=w)

    def ts(self, eng, out, in0, s1, s2, op0, op1=None, accum_out=None, reads=None, writes=None):
        kw = {}
        if op1 is not None:
            kw["op1"] = op1
        if accum_out is not None:
            kw["accum_out"] = accum_out
        r = [in0, s1, s2] if reads is None else reads
        w = [out, accum_out] if writes is None else writes
        return self.op(eng, lambda e: e.tensor_scalar(out, in0, s1, s2, op0, **kw), reads=r, writes=w)

    def stt(self, eng, out, in0, scalar, in1, op0, op1, reads=None, writes=None):
        r = [in0, scalar, in1] if reads is None else reads
        w = [out] if writes is None else writes
        return self.op(eng, lambda e: e.scalar_tensor_tensor(out, in0, scalar, in1, op0, op1), reads=r, writes=w)

    def copy(self, eng, out, in_, reads=None, writes=None):
        r = [in_] if reads is None else reads
        w = [out] if writes is None else writes
        if eng == "act":
            return self.op("act", lambda e: e.copy(out, in_), reads=r, writes=w)
        return self.op(eng, lambda e: e.tensor_copy(out, in_), reads=r, writes=w)

    def reduce(self, eng, out, in_, op, axis=AX.X, reads=None, writes=None):
        r = [in_] if reads is None else reads
        w = [out] if writes is None else writes
        return self.op(eng, lambda e: e.tensor_reduce(out, in_, axis, op), reads=r, writes=w)

    def memset(self, eng, ap, val):
        return self.op(eng, lambda e: e.memset(ap, val), reads=[], writes=[ap])

    def dma(self, q, out, in_, reads=None, writes=None, **kw):
        r = [in_] if reads is None else reads
        w = [out] if writes is None else writes
        X = self.op(q, lambda e: e.dma_start(out=out, in_=in_, **kw), reads=r, writes=w, is_dma=True)
        return X

    def finish(self, out_toks):
        return self.op("sp", lambda e: e.nop(), reads=out_toks, writes=[])

    def collective(self, kind, in_ap, out_ap, groups):
        return self.op("pool", lambda e: e.collective_compute(kind, ALU.bypass, replica_groups=groups,
                                                              ins=[in_ap], outs=[out_ap]),
                       reads=[in_ap], writes=[out_ap], is_dma=True, inc=1)

    def emit(self):
        nc = self.nc
        for e in ENGS:
            for o in self.streams[e]:
                if o.signal:
                    self.sigcount[e] += 1
                    o.sigval = self.sigcount[e]
            need = max(1, (self.sigcount[e] + SEM_EPOCH - 1) // SEM_EPOCH)
            while len(self.esems[e]) < need:
                self.esems[e].append(self.sem(f"e_{e}_{len(self.esems[e])}"))
        streams = self.streams
        esems = self.esems

        def body(ename):
            def f(eng):
                for o in streams[ename]:
                    for D in o.waits:
                        v = D.sigval - 1
                        eng.wait_ge(esems[D.eng][v // SEM_EPOCH], v % SEM_EPOCH + 1)
                    for (s, v) in o.dwaits:
                        eng.wait_ge(s, v)
                    ins = o.emit(eng)
                    if o.is_dma:
                        if o.inc == 16:
                            ins.then_inc(o.dma_sem, 16)
                        else:
                            ins.then_inc(o.dma_sem)
                    elif o.signal:
                        v = o.sigval - 1
                        ins.then_inc(esems[ename][v // SEM_EPOCH], 1)
            return f

        with nc.Block() as block:
            block.tensor(body("pe"))
            block.scalar(body("act"))
            block.vector(body("dve"))
            block.gpsimd(body("pool"))
            block.sync(body("sp"))
        self.streams = {e: [] for e in ENGS}

    def end_phase(self):
        X = Op("sp", lambda e: e.nop(), False)
        dk = self.dknow["sp"]
        for s_, v in self.all_sems.items():
            if dk.get(s_, 0) < v:
                X.dwaits.append((s_, v))
                dk[s_] = v
        X.clock = dict(self.clock["sp"])
        X.dknow = dict(dk)
        X.idx = self.nidx["sp"]
        self.nidx["sp"] += 1
        self.streams["sp"].append(X)
        self.emit()
        full = {e: self.nidx[e] - 1 for e in ENGS}
        for e in ENGS:
            self.clock[e] = dict(full)
            self.dknow[e] = dict(self.all_sems)
        for t in self.phase_toks:
            if t.dma_sem is not None:
                self.free_sems.append((t.dma_sem, t.dma_cnt))
                t.dma_sem = None
        self.phase_toks = []
        self.phase += 1


class Ctx:
    def __init__(self, P, ntt, nh=None, xT_chunks=16, light=False):
        self.P = P
        self.ntt = ntt
        nt = ntt * 128
        self.nt = nt
        if light:
            self.ident_f = P.sb("ident_f", [128, 128], F32)
            self.ident = P.sb("ident", [128, 128], BF16)
            self.bank = [P.ps(f"bank{i}", [128, 512], F32) for i in range(8)]
            P.memset("pool", self.ident_f[:], 0.0)
            P.op("pool", lambda e: e.affine_select(out=self.ident_f[:], in_=self.ident_f[:],
                                                   compare_op=ALU.not_equal, fill=1.0, base=0,
                                                   pattern=[[-1, 128]], channel_multiplier=1),
                 reads=[self.ident_f], writes=[self.ident_f])
            P.copy("dve", self.ident[:], self.ident_f[:])
            return
        self.h = [P.sb(f"h{t}", [128, D_MODEL], F32, ntok=4) for t in range(ntt if nh is None else nh)]
        self.xT = P.sb("xT", [128, xT_chunks, nt], BF16)
        self.xn = [P.sb(f"xn{i}", [128, D_MODEL], BF16) for i in range(2)]
        self.junk = P.sb("junk", [128, D_MODEL], BF16)
        self.ss = [P.sb(f"ss{i}", [128, 1], F32) for i in range(2)]
        self.rstd = [P.sb(f"rstd{i}", [128, 1], F32) for i in range(2)]
        self.wn = P.sb("wn_sb", [128, D_MODEL], F32)
        self.ident_f = P.sb("ident_f", [128, 128], F32)
        self.ident = P.sb("ident", [128, 128], BF16)
        self.bank = [P.ps(f"bank{i}", [128, 512], F32) for i in range(8)]
        self.rr = 0
        P.memset("pool", self.ident_f[:], 0.0)
        P.op("pool", lambda e: e.affine_select(out=self.ident_f[:], in_=self.ident_f[:],
                                               compare_op=ALU.not_equal, fill=1.0, base=0,
                                               pattern=[[-1, 128]], channel_multiplier=1),
             reads=[self.ident_f], writes=[self.ident_f])
        P.copy("dve", self.ident[:], self.ident_f[:])


def alloc_norm(P, C, xT_chunks=16, xT=True):
    if xT:
        C.xT = P.sb("xT", [128, xT_chunks, C.nt], BF16)
    C.xn = [P.sb(f"xn{i}", [128, D_MODEL], BF16) for i in range(2)]
    C.junk = P.sb("junk", [128, D_MODEL], BF16)
    C.ss = [P.sb(f"ss{i}", [128, 1], F32) for i in range(2)]
    C.rstd = [P.sb(f"rstd{i}", [128, 1], F32) for i in range(2)]
    C.wn = P.sb("wn_sb", [128, D_MODEL], F32)


def rmsnorm_to_xT(P, C, wnorm_ap, banks, extra=None):
    nc = P.nc
    P.dma("sp", C.wn[:], wnorm_ap.partition_broadcast(128))
    for t in range(C.ntt + (1 if extra is not None else 0)):
        i = t % 2
        h = C.h[t] if t < C.ntt else extra[0]
        P.memset("dve", C.ss[i][:], 0.0)
        P.act(C.junk[:], h[:], AF.Square, accum_out=C.ss[i][:])
        P.ts("dve", C.rstd[i][:], C.ss[i][:], 1.0 / D_MODEL, RMS_EPS, ALU.mult, ALU.add)
        P.act(C.rstd[i][:], C.rstd[i][:], AF.Sqrt)
        P.op("dve", lambda e, o=C.rstd[i]: e.reciprocal(o[:], o[:]), reads=[C.rstd[i]], writes=[C.rstd[i]])
        P.stt("dve", C.xn[i][:], h[:], C.rstd[i][:, 0:1], C.wn[:], ALU.mult, ALU.mult)
        for half in range(2):
            bk = banks[(2 * t + half) % len(banks)]
            tp = bk[:].bitcast(BF16)
            for j in range(8):
                dc = half * 8 + j
                P.transpose(tp[:, j * 128:(j + 1) * 128], C.xn[i][:, dc * 128:(dc + 1) * 128], C.ident[:])
            src = tp.rearrange("p (j n) -> p j n", j=8)
            if t < C.ntt:
                dst = C.xT[:, half * 8:(half + 1) * 8, t * 128:(t + 1) * 128]
            else:
                dst = extra[1][:, half * 8:(half + 1) * 8, :]
            if half == 0:
                P.copy("act", dst, src)
            else:
                P.copy("dve", dst, src)


def ffn(P, C, wnorm_ap, wg_ap, wu_ap, wd_ap, wbufs, G=2):
    nt = C.nt
    nth = nt // 512
    rmsnorm_to_xT(P, C, wnorm_ap, C.bank[4:8])
    wg_v = wg_ap.rearrange("(dc p) f -> p dc f", p=128)
    wu_v = wu_ap.rearrange("(dc p) f -> p dc f", p=128)
    wd_v = wd_ap.rearrange("(fc p) d -> p fc d", p=128)
    nfc = D_FF // 128
    ngroups = nfc // G
    gu_banks = C.bank[0:4]
    dn_banks = C.bank[4:8]
    gu_i = 0
    dn_i = 0
    for g in range(ngroups):
        wb = wbufs[g % len(wbufs)]
        f0 = g * G * 128
        P.dma("pool", wb["wg"][:], wg_v[:, :, f0:f0 + G * 128])
        P.dma("pool", wb["wu"][:], wu_v[:, :, f0:f0 + G * 128])
        P.dma("pool", wb["wd"][:], wd_v[:, g * G:(g + 1) * G, :])
        aT = wb["aT"]
        for fl in range(G):
            for th in range(nth):
                psg = gu_banks[gu_i % 4]
                psu = gu_banks[(gu_i + 1) % 4]
                gu_i += 2
                for dc in range(16):
                    P.matmul(psg[:], wb["wg"][:, dc, fl * 128:(fl + 1) * 128], C.xT[:, dc, th * 512:(th + 1) * 512],
                             start=(dc == 0), stop=(dc == 15))
                for dc in range(16):
                    P.matmul(psu[:], wb["wu"][:, dc, fl * 128:(fl + 1) * 128], C.xT[:, dc, th * 512:(th + 1) * 512],
                             start=(dc == 0), stop=(dc == 15))
                sg = wb["sg"][(fl * nth + th) % 2]
                P.act(sg[:], psg[:], AF.Silu)
                P.tt("dve", aT[:, fl, th * 512:(th + 1) * 512], psu[:], sg[:], ALU.mult)
        for t in range(C.ntt):
            for half in range(2):
                pd = [dn_banks[dn_i % 4], dn_banks[(dn_i + 1) % 4]]
                dn_i += 2
                for fl in range(G):
                    for j in range(2):
                        q = half * 2 + j
                        P.matmul(pd[j][:], aT[:, fl, t * 128:(t + 1) * 128], wb["wd"][:, fl, q * 512:(q + 1) * 512],
                                 start=(fl == 0), stop=(fl == G - 1))
                for j in range(2):
                    q = half * 2 + j
                    hq = C.h[t][:, q * 512:(q + 1) * 512]
                    tk = P.tok(C.h[t], q)
                    P.stt("dve", hq, pd[j][:], 0.5, hq, ALU.mult, ALU.add,
                          reads=[pd[j], tk], writes=[tk])


def make_wbufs(P, nt, G, nbuf, with_ffn=True):
    bufs = []
    for i in range(nbuf):
        b = {
            "wg": P.sb(f"wg{i}", [128, 16, G * 128], BF16),
            "wu": P.sb(f"wu{i}", [128, 16, G * 128], BF16),
            "wd": P.sb(f"wd{i}", [128, G, D_MODEL], BF16),
        }
        if with_ffn:
            b["aT"] = P.sb(f"aT{i}", [128, G, nt], BF16)
            b["sg"] = [P.sb(f"sg{i}_{k}", [128, 512], F32) for k in range(2)]
        bufs.append(b)
    return bufs


ADD_ENG = ["pool"]


class Ring:
    def __init__(self, P, name, shape, dtype, n):
        self.bufs = [P.sb(f"{name}{i}", shape, dtype) for i in range(n)]
        self.i = 0

    def next(self):
        b = self.bufs[self.i % len(self.bufs)]
        self.i += 1
        return b


def qkv_rope(P, C, wqkv_ap, cos_sb, sin_sb, rot_sb, qT_d, kT_d, v_d, wbufs, heads=range(8), do_qk=True, do_v=True, do_rope=True):
    nt = C.nt
    nth = nt // 512
    wv_ = wqkv_ap.rearrange("(dc p) f -> p dc f", p=128)
    st_b = Ring(P, "qk_b", [128, 512], BF16, 2)
    st_1 = Ring(P, "qk_t1", [128, 512], F32, 2)
    st_2 = Ring(P, "qk_t2", [128, 512], F32, 2)
    st_o = Ring(P, "qk_o", [128, 512], BF16, 3)
    st_v = Ring(P, "v_st", [128, C.ntt, 256], BF16, 2)
    qf = qT_d if callable(qT_d) else (lambda hd: qT_d[2 * hd:2 * hd + 2])
    kf = kT_d if callable(kT_d) else (lambda hd: kT_d[2 * hd:2 * hd + 2])
    v_list = v_d if isinstance(v_d, (list, tuple)) else [v_d]
    v_views = [v.rearrange("(t p) c -> p t c", p=128) for v in v_list]
    tpv = C.ntt // len(v_views)
    bi = 0
    for hd in heads:
        wb = wbufs[hd % len(wbufs)]
        wq = wb["wg"]
        wk = wb["wu"]
        wvv = wb["wd"][:].rearrange("p g (a b) -> p (g a) b", b=256)
        P.dma("pool", wq[:], wv_[:, :, hd * 256:(hd + 1) * 256])
        P.dma("pool", wk[:], wv_[:, :, 2048 + hd * 256:2048 + (hd + 1) * 256])
        P.dma("pool", wvv, wv_[:, :, 4096 + hd * 256:4096 + (hd + 1) * 256])
        for (w, dst) in (((wq, qf(hd)), (wk, kf(hd))) if do_qk else ()):
            for sub in range(2):
                for th in range(nth):
                    ps = C.bank[bi % 4]
                    psr = C.bank[4 + bi % 4]
                    bi += 1
                    for dc in range(16):
                        P.matmul(ps[:], w[:, dc, sub * 128:(sub + 1) * 128], C.xT[:, dc, th * 512:(th + 1) * 512],
                                 start=(dc == 0), stop=(dc == 15))
                    qb = st_b.next()
                    P.copy("act", qb[:], ps[:])
                    if not do_rope:
                        P.dma("sp", dst[sub, :, th * 512:(th + 1) * 512], qb[:])
                        continue
                    P.matmul(psr[:], rot_sb[:], qb[:], start=True, stop=True)
                    t1 = st_1.next()
                    t2 = st_2.next()
                    P.tt("dve", t1[:], ps[:], cos_sb[:, th * 512:(th + 1) * 512], ALU.mult)
                    P.tt("dve", t2[:], psr[:], sin_sb[:, th * 512:(th + 1) * 512], ALU.mult)
                    qo = st_o.next()
                    P.tt(ADD_ENG[0], qo[:], t1[:], t2[:], ALU.add)
                    P.dma("sp", dst[sub, :, th * 512:(th + 1) * 512], qo[:])
        vst = st_v.next()
        for t in (range(C.ntt) if do_v else ()):
            ps = C.bank[bi % 4]
            bi += 1
            for dc in range(16):
                P.matmul(ps[:, 0:256], C.xT[:, dc, t * 128:(t + 1) * 128], wvv[:, dc, :],
                         start=(dc == 0), stop=(dc == 15))
            P.copy("act", vst[:, t, :], ps[:, 0:256])
        if do_v:
            for vi, vv in enumerate(v_views):
                P.dma("sp", vv[:, :, hd * 256:(hd + 1) * 256], vst[:, vi * tpv:(vi + 1) * tpv, :])


def attn_core(P, C, qT_d, kTp_d, kTo_d, vp_d, vo_d, pbias_sb, nlam_sb, sublnw_sb, nkt_prev, nkt_own):
    nt = C.nt
    scale = 128.0 ** -0.5
    nq = nt // 256
    qT = P.sb("a_qT", [128, 2, nt], BF16)
    kT = P.sb("a_kT", [128, 2, (nkt_prev + nkt_own) * 128], BF16)
    V = P.sb("a_V", [128, nkt_prev + nkt_own, 258], BF16)
    pTr = Ring(P, "a_pT", [128, 512], BF16, 3)
    msk = [P.sb(f"a_msk{r}", [128, 2, 256], BF16) for r in range(2)]
    o_sb = Ring(P, "a_o", [128, 256], F32, 2)
    on_sb = Ring(P, "a_on", [128, 256], BF16, 2)
    sq_sb = P.sb("a_sq", [128, 256], F32)
    sm = Ring(P, "a_sm", [128, 8], F32, 2)
    for r in range(2):
        P.memset("pool", msk[r][:], 1.0)
        P.op("pool", lambda e, r=r: e.affine_select(out=msk[r][:], in_=msk[r][:], compare_op=ALU.is_ge, fill=0.0,
                                                    base=-128 * r, pattern=[[0, 2], [1, 256]],
                                                    channel_multiplier=-1),
             reads=[msk[r]], writes=[msk[r]])
    P.memset("pool", V[:, :, 256:258], 1.0)
    qf = qT_d if callable(qT_d) else (lambda hd: qT_d[2 * hd:2 * hd + 2])
    kpf = kTp_d if callable(kTp_d) else (lambda hd: kTp_d[2 * hd:2 * hd + 2])
    kof = kTo_d if callable(kTo_d) else (lambda hd: kTo_d[2 * hd:2 * hd + 2])
    vp_vs = [v.rearrange("(t p) c -> p t c", p=128) for v in (vp_d if isinstance(vp_d, (list, tuple)) else [vp_d])]
    vo_vs = [v.rearrange("(t p) c -> p t c", p=128) for v in (vo_d if isinstance(vo_d, (list, tuple)) else [vo_d])]
    s_banks = C.bank[0:3]
    o_banks = C.bank[3:7]
    t_bank = C.bank[7]
    si = 0
    for hd in range(8):
        P.dma("sp", qT[:], qf(hd).rearrange("s p t -> p s t"))
        if nkt_prev:
            P.dma("act", kT[:, :, 0:nkt_prev * 128], kpf(hd).rearrange("s p t -> p s t"))
            n1 = nkt_prev // len(vp_vs)
            for vi, vv in enumerate(vp_vs):
                P.dma("sp", V[:, vi * n1:(vi + 1) * n1, 0:256], vv[:, :, hd * 256:(hd + 1) * 256])
        P.dma("act", kT[:, :, nkt_prev * 128:], kof(hd).rearrange("s p t -> p s t"))
        n2 = nkt_own // len(vo_vs)
        for vi, vv in enumerate(vo_vs):
            P.dma("sp", V[:, nkt_prev + vi * n2:nkt_prev + (vi + 1) * n2, 0:256], vv[:, :, hd * 256:(hd + 1) * 256])
        for qb in range(nq):
            kts = [(k, "prev") for k in range(nkt_prev)]
            kts += [(nkt_prev + k, "full") for k in range(2 * qb)]
            kts += [(nkt_prev + 2 * qb, "d0"), (nkt_prev + 2 * qb + 1, "d1")]
            for i, (kt, kind) in enumerate(kts):
                psS = s_banks[si % 3]
                si += 1
                for sh in range(2):
                    P.matmul(psS[:, sh * 256:(sh + 1) * 256], kT[:, sh, kt * 128:(kt + 1) * 128],
                             qT[:, sh, qb * 256:(qb + 1) * 256], start=True, stop=True)
                pT = pTr.next()
                if kind == "prev":
                    P.act(pT[:], psS[:], AF.Exp, bias=pbias_sb[:, 0:1], scale=scale)
                else:
                    P.act(pT[:], psS[:], AF.Exp, scale=scale)
                if kind in ("d0", "d1"):
                    m = msk[0] if kind == "d0" else msk[1]
                    P.tt("pool", pT[:], pT[:], m[:].rearrange("p s q -> p (s q)"), ALU.mult)
                for qs in range(2):
                    for sh in range(2):
                        P.matmul(o_banks[qs * 2 + sh][:, 0:257],
                                 pT[:, sh * 256 + qs * 128:sh * 256 + (qs + 1) * 128], V[:, kt, 0:257],
                                 start=(i == 0), stop=(i == len(kts) - 1))
            for qs in range(2):
                O1 = o_banks[qs * 2]
                O2 = o_banks[qs * 2 + 1]
                s = sm.next()
                o = o_sb.next()
                P.op("dve", lambda e, s=s, O1=O1: e.reciprocal(s[:, 0:1], O1[:, 256:257]), reads=[O1], writes=[s])
                P.op("dve", lambda e, s=s, O2=O2: e.reciprocal(s[:, 1:2], O2[:, 256:257]), reads=[O2, s], writes=[s])
                P.tt("dve", s[:, 2:3], s[:, 1:2], nlam_sb[:, 0:1], ALU.mult)
                P.ts("dve", o[:], O1[:, 0:256], s[:, 0:1], None, ALU.mult)
                P.stt("dve", o[:], O2[:, 0:256], s[:, 2:3], o[:], ALU.mult, ALU.add)
                P.memset("dve", s[:, 3:4], 0.0)
                P.act(sq_sb[:], o[:], AF.Square, accum_out=s[:, 3:4])
                P.ts("dve", s[:, 4:5], s[:, 3:4], 1.0 / 256.0, RMS_EPS, ALU.mult, ALU.add)
                P.act(s[:, 5:6], s[:, 4:5], AF.Sqrt)
                P.op("dve", lambda e, s=s: e.reciprocal(s[:, 6:7], s[:, 5:6]), reads=[s], writes=[s])
                on = on_sb.next()
                P.stt("dve", on[:], o[:], s[:, 6:7], sublnw_sb[:], ALU.mult, ALU.mult)
                tp = t_bank[:].bitcast(BF16)
                tt_i = qb * 2 + qs
                for j in range(2):
                    P.transpose(tp[:, j * 128:(j + 1) * 128], on[:, j * 128:(j + 1) * 128], C.ident[:])
                P.copy("act", C.xT[:, 2 * hd:2 * hd + 2, tt_i * 128:(tt_i + 1) * 128],
                       tp[:, 0:256].rearrange("p (j n) -> p j n", j=2))


def out_proj(P, C, wo_ap, wo_sb, kchunks=16):
    wv_ = wo_ap.rearrange("(c p) f -> p c f", p=128)
    bi = 0
    for q in range(4):
        wq = wo_sb[q % len(wo_sb)]
        P.dma("pool", wq[:], wv_[:, :, q * 512:(q + 1) * 512])
        for t in range(C.ntt):
            ps = C.bank[bi % 4]
            bi += 1
            for c in range(kchunks):
                P.matmul(ps[:], C.xT[:, c, t * 128:(t + 1) * 128], wq[:, c, :], start=(c == 0), stop=(c == kchunks - 1))
            hq = C.h[t][:, q * 512:(q + 1) * 512]
            tk = P.tok(C.h[t], q)
            P.tt("dve", hq, ps[:], hq, ALU.add, reads=[ps, tk], writes=[tk])


def final_norm(P, C, w_ap, out_v):
    P.dma("sp", C.wn[:], w_ap.partition_broadcast(128))
    for t in range(C.ntt):
        i = t % 2
        h = C.h[t]
        P.memset("dve", C.ss[i][:], 0.0)
        P.act(C.junk[:], h[:], AF.Square, accum_out=C.ss[i][:])
        P.ts("dve", C.rstd[i][:], C.ss[i][:], 1.0 / D_MODEL, RMS_EPS, ALU.mult, ALU.add)
        P.act(C.rstd[i][:], C.rstd[i][:], AF.Sqrt)
        P.op("dve", lambda e, o=C.rstd[i]: e.reciprocal(o[:], o[:]), reads=[C.rstd[i]], writes=[C.rstd[i]])
        P.stt("dve", h[:], h[:], C.rstd[i][:, 0:1], C.wn[:], ALU.mult, ALU.mult)
        P.dma("sp", out_v[t], h[:])


def build_ffn_prog(ntt=8, G=2, nbuf=3, with_final=False):
    nc = bass.Bass("TRN2", target_bir_lowering=False)
    nt = ntt * 128
    with ExitStack() as stack:
        P = Prog(nc, stack)
        h_in = nc.dram_tensor("h_in", [nt, D_MODEL], F32, kind="ExternalInput").ap()
        wn = nc.dram_tensor("wn", [D_MODEL], F32, kind="ExternalInput").ap()
        wg = nc.dram_tensor("wg", [D_MODEL, D_FF], F32, kind="ExternalInput").ap()
        wu = nc.dram_tensor("wu", [D_MODEL, D_FF], F32, kind="ExternalInput").ap()
        wd = nc.dram_tensor("wd", [D_FF, D_MODEL], F32, kind="ExternalInput").ap()
        if with_final:
            wf = nc.dram_tensor("wf", [D_MODEL], F32, kind="ExternalInput").ap()
        h_out_t = P.dram("h_out", [nt, D_MODEL], F32, kind="ExternalOutput")
        h_out = h_out_t.ap()
        C = Ctx(P, ntt)
        hv = h_in.rearrange("(t p) d -> t p d", p=128)
        ov = h_out.rearrange("(t p) d -> t p d", p=128)
        for t in range(ntt):
            P.dma("sp" if t % 2 == 0 else "act", C.h[t][:], hv[t])
        wbufs = make_wbufs(P, nt, G, nbuf)
        ffn(P, C, wn, wg, wu, wd, wbufs, G=G)
        if with_final:
            final_norm(P, C, wf, ov)
        else:
            for t in range(ntt):
                P.dma("sp", ov[t], C.h[t][:])
        P.finish(P.tok(h_out_t))
        P.emit()
    return nc


def build_qkv_prog(ntt=8, **dbg):
    nc = bass.Bass("TRN2", target_bir_lowering=False)
    nt = ntt * 128
    with ExitStack() as stack:
        P = Prog(nc, stack)
        h_in = nc.dram_tensor("h_in", [nt, D_MODEL], F32, kind="ExternalInput").ap()
        wn = nc.dram_tensor("wn", [D_MODEL], F32, kind="ExternalInput").ap()
        wqkv = nc.dram_tensor("wqkv", [D_MODEL, 3 * D_MODEL], F32, kind="ExternalInput").ap()
        cos_d = nc.dram_tensor("cosT", [128, nt], F32, kind="ExternalInput").ap()
        sin_d = nc.dram_tensor("sinT", [128, nt], F32, kind="ExternalInput").ap()
        rot_d = nc.dram_tensor("rotT", [128, 128], BF16, kind="ExternalInput").ap()
        qT_t = P.dram("qT", [16, 128, nt], BF16, kind="ExternalOutput")
        kT_t = P.dram("kT", [16, 128, nt], BF16, kind="ExternalOutput")
        v_t = P.dram("v", [nt, D_MODEL], BF16, kind="ExternalOutput")
        C = Ctx(P, ntt)
        hv = h_in.rearrange("(t p) d -> t p d", p=128)
        for t in range(ntt):
            P.dma("sp" if t % 2 == 0 else "act", C.h[t][:], hv[t])
        cos_sb = P.sb("cos_sb", [128, nt], F32)
        sin_sb = P.sb("sin_sb", [128, nt], F32)
        rot_sb = P.sb("rot_sb", [128, 128], BF16)
        P.dma("sp", cos_sb[:], cos_d)
        P.dma("sp", sin_sb[:], sin_d)
        P.dma("sp", rot_sb[:], rot_d)
        wbufs = make_wbufs(P, nt, 2, 2, with_ffn=False)
        rmsnorm_to_xT(P, C, wn, C.bank[4:8])
        qkv_rope(P, C, wqkv, cos_sb, sin_sb, rot_sb, qT_t.ap(), kT_t.ap(), v_t.ap(), wbufs, **dbg)
        P.finish(P.tok(qT_t) + P.tok(kT_t) + P.tok(v_t))
        P.emit()
    return nc


def build_att_prog(ntt=8, lambda_init=0.2):
    nc = bass.Bass("TRN2", target_bir_lowering=False)
    nt = ntt * 128
    with ExitStack() as stack:
        P = Prog(nc, stack)
        h_in = nc.dram_tensor("h_in", [nt, D_MODEL], F32, kind="ExternalInput").ap()
        qT_d = nc.dram_tensor("qT", [16, 128, nt], BF16, kind="ExternalInput").ap()
        kTp_d = nc.dram_tensor("kTp", [16, 128, nt], BF16, kind="ExternalInput").ap()
        kTo_d = nc.dram_tensor("kTo", [16, 128, nt], BF16, kind="ExternalInput").ap()
        vp_d = nc.dram_tensor("vp", [nt, D_MODEL], BF16, kind="ExternalInput").ap()
        vo_d = nc.dram_tensor("vo", [nt, D_MODEL], BF16, kind="ExternalInput").ap()
        pb_d = nc.dram_tensor("pbias", [128, 1], F32, kind="ExternalInput").ap()
        lam_d = [nc.dram_tensor(n, [128], F32, kind="ExternalInput").ap() for n in ("lq1", "lk1", "lq2", "lk2")]
        sub_d = nc.dram_tensor("subln", [256], F32, kind="ExternalInput").ap()
        wo_d = nc.dram_tensor("wo", [D_MODEL, D_MODEL], F32, kind="ExternalInput").ap()
        h_out_t = P.dram("h_out", [nt, D_MODEL], F32, kind="ExternalOutput")
        C = Ctx(P, ntt)
        hv = h_in.rearrange("(t p) d -> t p d", p=128)
        ov = h_out_t.ap().rearrange("(t p) d -> t p d", p=128)
        for t in range(ntt):
            P.dma("sp" if t % 2 == 0 else "act", C.h[t][:], hv[t])
        pbias = P.sb("pbias_sb", [128, 1], F32)
        P.dma("sp", pbias[:], pb_d)
        lt = [P.sb(f"lam{i}", [128, 128], F32) for i in range(4)]
        for i in range(4):
            P.dma("sp", lt[i][:], lam_d[i].partition_broadcast(128))
        ls = P.sb("lam_s", [128, 4], F32)
        P.tt("dve", lt[0][:], lt[0][:], lt[1][:], ALU.mult)
        P.tt("dve", lt[2][:], lt[2][:], lt[3][:], ALU.mult)
        P.reduce("dve", ls[:, 0:1], lt[0][:], ALU.add)
        P.reduce("dve", ls[:, 1:2], lt[2][:], ALU.add)
        P.act(ls[:, 0:2], ls[:, 0:2], AF.Exp)
        nlam = P.sb("nlam", [128, 1], F32)
        P.tt("dve", ls[:, 2:3], ls[:, 1:2], ls[:, 0:1], ALU.subtract)
        P.ts("dve", nlam[:], ls[:, 2:3], -float(lambda_init), None, ALU.add)
        sublnw = P.sb("sublnw", [128, 256], F32)
        P.dma("sp", sublnw[:], sub_d.partition_broadcast(128))
        P.ts("dve", sublnw[:], sublnw[:], 1.0 - float(lambda_init), None, ALU.mult)
        attn_core(P, C, qT_d, kTp_d, kTo_d, vp_d, vo_d, pbias, nlam, sublnw, ntt, ntt)
        wo_sb = [P.sb(f"wo{i}", [128, 16, 512], BF16) for i in range(2)]
        out_proj(P, C, wo_d, wo_sb)
        for t in range(ntt):
            P.dma("sp", ov[t], C.h[t][:])
        P.finish(P.tok(h_out_t))
        P.emit()
    return nc


def gdn_inproj(P, C, xTh, w_in, convw_ap, alog_ap, dtb_ap, qT_d, kT_d, vT_d, z_d, beta_d, g_d, wbufs, halo0=125):
    nt = C.nt
    wv_ = w_in.rearrange("(dc p) f -> p dc f", p=128)
    cwj = [P.sb(f"g_cw{j}", [128, 64], F32) for j in range(4)]
    cw_raw = P.sb("g_cwraw", [64, 4, 128], F32)
    P.dma("sp", cw_raw[:], convw_ap.rearrange("j (cb p) -> cb j p", p=128))
    for j in range(4):
        P.transpose(C.bank[j][:, 0:64], cw_raw[:, j, :], C.ident_f[0:64, 0:64])
        P.copy("dve", cwj[j][:], C.bank[j][:, 0:64])
    ones_bf = P.sb("g_ones", [128, 128], BF16)
    P.memset("pool", ones_bf[:], 1.0)
    pre_r = Ring(P, "g_pre", [128, nt + 3], F32, 2)
    acc_r = Ring(P, "g_acc", [128, nt], F32, 2)
    sil_r = Ring(P, "g_sil", [128, nt], F32, 2)
    sq_r = Ring(P, "g_sq", [128, nt], BF16, 2)
    rs_r = Ring(P, "g_rs", [128, nt], F32, 1)
    ob_r = Ring(P, "g_ob", [128, nt], BF16, 2)
    bi = 0
    for cb2 in range(32):
        wb = wbufs[cb2 % len(wbufs)]
        w = wb["wg"] if (cb2 // len(wbufs)) % 2 == 0 else wb["wu"]
        P.dma("pool", w[:], wv_[:, :, cb2 * 256:(cb2 + 1) * 256])
        for sub in range(2):
            cb = cb2 * 2 + sub
            ps0 = C.bank[bi % 2 * 3]
            ps1 = C.bank[bi % 2 * 3 + 1]
            psh = C.bank[bi % 2 * 3 + 2]
            bi += 1
            for dc in range(16):
                lw = w[:, dc, sub * 128:(sub + 1) * 128]
                P.matmul(ps0[:], lw, C.xT[:, dc, 0:512], start=(dc == 0), stop=(dc == 15))
                P.matmul(ps1[:], lw, C.xT[:, dc, 512:1024], start=(dc == 0), stop=(dc == 15))
                P.matmul(psh[:, 0:3], lw, xTh[:, dc, halo0:halo0 + 3], start=(dc == 0), stop=(dc == 15))
            pre = pre_r.next()
            P.copy("act", pre[:, 0:3], psh[:, 0:3])
            P.copy("act", pre[:, 3:515], ps0[:])
            P.copy("act", pre[:, 515:1027], ps1[:])
            acc = acc_r.next()
            P.ts("dve", acc[:], pre[:, 0:nt], cwj[0][:, cb:cb + 1], None, ALU.mult)
            P.stt("dve", acc[:], pre[:, 1:nt + 1], cwj[1][:, cb:cb + 1], acc[:], ALU.mult, ALU.add)
            P.stt("dve", acc[:], pre[:, 2:nt + 2], cwj[2][:, cb:cb + 1], acc[:], ALU.mult, ALU.add)
            P.stt("dve", acc[:], pre[:, 3:nt + 3], cwj[3][:, cb:cb + 1], acc[:], ALU.mult, ALU.add)
            if cb >= 32:
                ob = ob_r.next()
                P.act(ob[:], acc[:], AF.Silu)
                P.dma("sp", vT_d[cb - 32], ob[:])
                continue
            sil = sil_r.next()
            P.act(sil[:], acc[:], AF.Silu)
            sq = sq_r.next()
            P.tt("pool", sq[:], sil[:], sil[:], ALU.mult)
            rs = rs_r.next()
            for th in range(nt // 512):
                pss = C.bank[6 + th % 2]
                P.matmul(pss[:], ones_bf[:], sq[:, th * 512:(th + 1) * 512], start=True, stop=True)
                P.ts("dve", rs[:, th * 512:(th + 1) * 512], pss[:], 1e-6, None, ALU.add)
            P.act(rs[:], rs[:], AF.Sqrt)
            P.op("dve", lambda e, rs=rs: e.reciprocal(rs[:], rs[:]), reads=[rs], writes=[rs])
            ob = ob_r.next()
            P.stt("dve", ob[:], sil[:], (128.0 ** -0.5) if cb < 16 else 1.0, rs[:], ALU.mult, ALU.mult)
            P.dma("sp", (qT_d[cb] if cb < 16 else kT_d[cb - 16]), ob[:])
    zst = Ring(P, "g_zst", [128, 512], BF16, 3)
    for zq in range(8):
        wb = wbufs[zq % len(wbufs)]
        wz = wb["wd"][:].rearrange("p g (a b) -> p (g a) b", b=512) if False else None
        wzt = zw_tiles[zq % 2]
        P.dma("pool", wzt[:], wv_[:, :, 8192 + zq * 512:8192 + (zq + 1) * 512])
        for t in range(C.ntt):
            ps = C.bank[bi % 4]
            bi += 1
            for dc in range(16):
                P.matmul(ps[:], C.xT[:, dc, t * 128:(t + 1) * 128], wzt[:, dc, :], start=(dc == 0), stop=(dc == 15))
            zs = zst.next()
            P.copy("act", zs[:], ps[:])
            P.dma("sp", z_d[t * 128:(t + 1) * 128, zq * 512:(zq + 1) * 512], zs[:])
    wba = P.sb("g_wba", [128, 16, 64], BF16)
    P.dma("pool", wba[:], wv_[:, :, 12288:12352])
    nega = P.sb("g_nega", [128, 32], F32)
    dtb = P.sb("g_dtb", [128, 32], F32)
    P.dma("sp", nega[:], alog_ap.partition_broadcast(128))
    P.dma("sp", dtb[:], dtb_ap.partition_broadcast(128))
    P.act(nega[:], nega[:], AF.Exp)
    P.ts("dve", nega[:], nega[:], -1.0, None, ALU.mult)
    gst = Ring(P, "g_gst", [128, 64], F32, 2)
    gtm = Ring(P, "g_gtm", [128, 32], F32, 2)
    for t in range(C.ntt):
        ps = C.bank[bi % 4]
        bi += 1
        for dc in range(16):
            P.matmul(ps[:, 0:64], C.xT[:, dc, t * 128:(t + 1) * 128], wba[:, dc, :], start=(dc == 0), stop=(dc == 15))
        o = gst.next()
        tm = gtm.next()
        P.tt("dve", tm[:], ps[:, 32:64], dtb[:], ALU.add)
        P.act(o[:, 0:32], ps[:, 0:32], AF.Sigmoid)
        P.act(tm[:], tm[:], AF.Exp)
        P.ts("dve", tm[:], tm[:], 1.0, None, ALU.add)
        P.act(tm[:], tm[:], AF.Ln)
        P.tt("dve", o[:, 32:64], tm[:], nega[:], ALU.mult)
        P.dma("sp", beta_d[t * 128:(t + 1) * 128, :], o[:, 0:32])
        P.dma("sp", g_d[t * 128:(t + 1) * 128, :], o[:, 32:64])


def build_g1_prog(ntt=8):
    nc = bass.Bass("TRN2", target_bir_lowering=False)
    nt = ntt * 128
    with ExitStack() as stack:
        P = Prog(nc, stack)
        h_in = nc.dram_tensor("h_in", [nt, D_MODEL], F32, kind="ExternalInput").ap()
        h_halo = nc.dram_tensor("h_halo", [128, D_MODEL], F32, kind="ExternalInput").ap()
        wn = nc.dram_tensor("wn", [D_MODEL], F32, kind="ExternalInput").ap()
        w_in = nc.dram_tensor("w_in", [D_MODEL, 12352], F32, kind="ExternalInput").ap()
        convw = nc.dram_tensor("conv_w", [4, 8192], F32, kind="ExternalInput").ap()
        alog = nc.dram_tensor("a_log", [32], F32, kind="ExternalInput").ap()
        dtb = nc.dram_tensor("dt_bias", [32], F32, kind="ExternalInput").ap()
        qT_t = P.dram("qT", [16, 128, nt], BF16, kind="ExternalOutput")
        kT_t = P.dram("kT", [16, 128, nt], BF16, kind="ExternalOutput")
        vT_t = P.dram("vT", [32, 128, nt], BF16, kind="ExternalOutput")
        z_t = P.dram("z", [nt, 4096], BF16, kind="ExternalOutput")
        b_t = P.dram("beta", [nt, 32], F32, kind="ExternalOutput")
        g_t = P.dram("g", [nt, 32], F32, kind="ExternalOutput")
        C = Ctx(P, ntt, nh=2)
        hh = P.sb("hh", [128, D_MODEL], F32)
        xTh = P.sb("xTh", [128, 16, 128], BF16)
        hv = h_in.rearrange("(t p) d -> t p d", p=128)
        hs = C.h
        C.h = [hs[t % 2] for t in range(ntt)]
        P.dma("sp", hh[:], h_halo)
        wbufs = make_wbufs(P, nt, 2, 2, with_ffn=False)
        global zw_tiles
        zw_tiles = [P.sb(f"g_wz{i}", [128, 16, 512], BF16) for i in range(2)]
        rmsnorm_stream(P, C, wn, C.bank[4:8], hv, (hh, xTh))
        gdn_inproj(P, C, xTh, w_in, convw, alog, dtb, qT_t.ap(), kT_t.ap(), vT_t.ap(), z_t.ap(), b_t.ap(), g_t.ap(), wbufs)
        P.finish(P.tok(qT_t) + P.tok(kT_t) + P.tok(vT_t) + P.tok(z_t) + P.tok(b_t) + P.tok(g_t))
        P.emit()
    return nc


def rmsnorm_stream(P, C, wnorm_ap, banks, hv, extra):
    P.dma("sp", C.wn[:], wnorm_ap.partition_broadcast(128))
    for t in range(C.ntt + 1):
        i = t % 2
        if t < C.ntt:
            h = C.h[t]
            P.dma("sp" if t % 2 == 0 else "act", h[:], hv[t])
        else:
            h = extra[0]
        P.memset("dve", C.ss[i][:], 0.0)
        P.act(C.junk[:], h[:], AF.Square, accum_out=C.ss[i][:])
        P.ts("dve", C.rstd[i][:], C.ss[i][:], 1.0 / D_MODEL, RMS_EPS, ALU.mult, ALU.add)
        P.act(C.rstd[i][:], C.rstd[i][:], AF.Sqrt)
        P.op("dve", lambda e, o=C.rstd[i]: e.reciprocal(o[:], o[:]), reads=[C.rstd[i]], writes=[C.rstd[i]])
        P.stt("dve", C.xn[i][:], h[:], C.rstd[i][:, 0:1], C.wn[:], ALU.mult, ALU.mult)
        for half in range(2):
            bk = banks[(2 * t + half) % len(banks)]
            tp = bk[:].bitcast(BF16)
            for j in range(8):
                dc = half * 8 + j
                P.transpose(tp[:, j * 128:(j + 1) * 128], C.xn[i][:, dc * 128:(dc + 1) * 128], C.ident[:])
            src = tp.rearrange("p (j n) -> p j n", j=8)
            if t < C.ntt:
                dst = C.xT[:, half * 8:(half + 1) * 8, t * 128:(t + 1) * 128]
            else:
                dst = extra[1][:, half * 8:(half + 1) * 8, :]
            P.copy("act" if half == 0 else "dve", dst, src)


def view(ap, dims):
    return bass.AP(ap.tensor, ap.offset, [list(ap.ap[0])] + [list(d) for d in dims])


def gdn_chunks(P, C, qT_d, kT_d, vT_d, z_d, beta_d, g_d, S0_d, normw_ap, on_d, Sfin_d, nchunks=16, s0_flag=None, state_only=False):
    I64 = C.ident_f[0:64, 0:64]
    ones64 = P.sb("c_ones64", [64, 64], F32)
    neg64 = P.sb("c_neg64", [64, 64], F32)
    onesall = P.sb("c_onesall", [64, 128], F32)
    triu = P.sb("c_triu", [64, 64], F32)
    strict = P.sb("c_strict", [64, 64], F32)
    normw = P.sb("c_normw", [64, 128], F32)
    P.memset("pool", ones64[:], 1.0)
    P.memset("pool", neg64[:], -1.0)
    P.memset("pool", onesall[:], 1.0)
    for (t, base) in ((triu, 0), (strict, -1)):
        P.memset("pool", t[:], 1.0 if t is triu else -1.0)
        P.op("pool", lambda e, t=t, base=base: e.affine_select(out=t[:], in_=t[:], compare_op=ALU.is_ge, fill=0.0,
                                                               base=base, pattern=[[1, 64]], channel_multiplier=-1),
             reads=[t], writes=[t])
    P.dma("sp", normw[:], normw_ap.partition_broadcast(64))
    S = P.sb("S", [128, 32, 128], F32, ntok=32)
    Sb = P.sb("Sb", [128, 32, 128], BF16, ntok=32)
    if S0_d is None:
        P.memset("pool", S[:], 0.0)
    else:
        P.dma("sp", S[:], S0_d.rearrange("h p d -> p h d"))
        if s0_flag is not None:
            P.ts("dve", S[:], S[:], s0_flag[:, 0:1], None, ALU.mult)
    P.copy("pool", Sb[:], S[:])
    kT4 = P.sb("kT4", [128, 16, 256], BF16)
    qT4 = P.sb("qT4", [128, 16, 256], BF16)
    vT4 = P.sb("vT4", [128, 32, 256], BF16)
    onT4 = P.sb("onT4", [128, 32, 256], BF16)
    zc = P.sb("zc", [64, 4096], BF16)
    bc = P.sb("bc", [64, 32], F32)
    gcl = P.sb("gcl", [64, 32], F32)
    gt = P.sb("gates", [64, 5, 32], F32)
    gl = P.sb("gl", [128, 32], F32)
    k_tm = P.sb("k_tm", [64, 16, 128], BF16)
    v_tm = P.sb("v_tm", [64, 32, 128], BF16)
    kgl = P.sb("kgl", [64, 32, 128], BF16)
    tmp = P.sb("tmpA", [64, 4096], F32)
    Dg = tmp[:, 0:2048]
    E = tmp[:, 2048:4096]
    PA = [P.sb(f"PA{i}", [64, 32, 64], BF16) for i in range(2)]
    PT = [P.sb(f"PT{i}", [64, 32, 64], BF16) for i in range(2)]
    X = P.sb("Xinv", [64, 32, 64], F32)
    Xb = P.sb("Xb", [64, 32, 64], BF16)
    QKd = P.sb("QKd", [64, 32, 64], BF16)
    o_c = P.sb("o_c", [64, 32, 128], F32, ntok=32)
    zs = P.sb("zs", [64, 4096], BF16)
    on_sb = P.sb("on_sb", [64, 4096], BF16)
    nrm = P.sb("nrm", [64, 2, 32], F32)
    r_r = Ring(P, "r_sb", [64, 128], BF16, 6)
    qs_r = Ring(P, "qs_sb", [64, 128], F32, 6)
    vn_r = Ring(P, "vn_sb", [64, 128], BF16, 6)
    bk = C.bank
    for n in range(nchunks):
        co = (n % 4) * 64
        if n % 4 == 0:
            sl = slice(n * 64, n * 64 + 256)
            P.dma("sp", kT4[:], kT_d[:, :, sl].rearrange("k p t -> p k t"))
            if not state_only:
                P.dma("act", qT4[:], qT_d[:, :, sl].rearrange("k p t -> p k t"))
            P.dma("sp", vT4[:], vT_d[:, :, sl].rearrange("k p t -> p k t"))
        if not state_only:
            P.dma("act", zc[:], z_d[n * 64:(n + 1) * 64, :])
        P.dma("sp", bc[:], beta_d[n * 64:(n + 1) * 64, :])
        P.dma("sp", gcl[:], g_d[n * 64:(n + 1) * 64, :])
        gp = bk[6]
        P.matmul(gp[0:64, 0:32], triu[:], gcl[:], start=True, stop=True)
        P.matmul(gp[:, 32:64], onesall[:], gcl[:], start=True, stop=True)
        gc = gt[:, 0, :]
        P.copy("dve", gc, gp[0:64, 0:32])
        P.act(gt[:, 1, :], gp[0:64, 0:32], AF.Exp)
        P.act(gt[:, 2, :], gp[0:64, 0:32], AF.Exp)
        P.ts("dve", gt[:, 2, :], gt[:, 2, :], -1.0, None, ALU.mult)
        P.tt("dve", gt[:, 4, :], gp[0:64, 32:64], gc, ALU.subtract)
        P.act(gt[:, 3, :], gt[:, 4, :], AF.Exp)
        P.ts("dve", gt[:, 4, :], gc, -1.0, None, ALU.mult)
        P.act(gl[:], gp[:, 32:64], AF.Exp)
        for blk in range(2):
            tp = bk[7][:].bitcast(BF16)
            for j in range(8):
                kh = blk * 8 + j
                P.transpose(tp[0:64, j * 128:(j + 1) * 128], kT4[:, kh, co:co + 64], C.ident[:])
            P.copy("act", k_tm[:, blk * 8:(blk + 1) * 8, :], tp[0:64, :].rearrange("p (j d) -> p j d", j=8))
        for blk in range(4):
            tp = bk[6 + blk % 2][:].bitcast(BF16)
            for j in range(8):
                hh = blk * 8 + j
                P.transpose(tp[0:64, j * 128:(j + 1) * 128], vT4[:, hh, co:co + 64], C.ident[:])
            P.copy("act" if blk % 2 == 0 else "dve", v_tm[:, blk * 8:(blk + 1) * 8, :],
                   tp[0:64, :].rearrange("p (j d) -> p j d", j=8))
        P.tt("dve", view(kgl[:], [[256, 16], [128, 2], [1, 128]]), view(k_tm[:], [[128, 16], [0, 2], [1, 128]]),
             view(gt[:, 3, :], [[2, 16], [1, 2], [0, 128]]), ALU.mult)
        P.tt("dve", view(Dg, [[64, 32], [1, 64]]), view(I64, [[0, 32], [1, 64]]), view(gc, [[1, 32], [0, 64]]), ALU.mult)
        for gq in range(4):
            dps = bk[6 + gq % 2]
            P.matmul(dps[0:64, :], ones64[:], Dg[:, gq * 512:(gq + 1) * 512], start=True, stop=True)
            Eg = E[:, gq * 512:(gq + 1) * 512]
            for j in range(8):
                h = gq * 8 + j
                P.ts("dve", Eg[:, j * 64:(j + 1) * 64], dps[0:64, j * 64:(j + 1) * 64], gt[:, 4, h:h + 1], 0.0, ALU.add, ALU.min)
            P.act(Eg, Eg, AF.Exp)
            P.tt("pool", view(Eg, [[64, 8], [1, 64]]), view(Eg, [[64, 8], [1, 64]]), view(triu[:], [[0, 8], [1, 64]]), ALU.mult)
        for half in range(2):
            kkp = bk[6]
            qkp = bk[7]
            for j in range(8):
                kh = half * 8 + j
                P.matmul(kkp[0:64, j * 64:(j + 1) * 64], kT4[:, kh, co:co + 64], kT4[:, kh, co:co + 64], start=True, stop=True)
                if not state_only:
                    P.matmul(qkp[0:64, j * 64:(j + 1) * 64], kT4[:, kh, co:co + 64], qT4[:, kh, co:co + 64], start=True, stop=True)
            hs = slice(half * 16, (half + 1) * 16)
            Eh = view(E[:, half * 1024:(half + 1) * 1024], [[128, 8], [64, 2], [1, 64]])
            if not state_only:
                P.tt("dve", view(QKd[:, hs, :], [[128, 8], [64, 2], [1, 64]]), view(qkp[0:64, :], [[64, 8], [0, 2], [1, 64]]), Eh, ALU.mult)
            P0h = view(PA[0][:, hs, :], [[128, 8], [64, 2], [1, 64]])
            P.tt("dve", P0h, view(kkp[0:64, :], [[64, 8], [0, 2], [1, 64]]), Eh, ALU.mult)
            P.tt("pool", PA[0][:, hs, :], PA[0][:, hs, :], view(bc[:, hs], [[1, 16], [0, 64]]), ALU.mult)
            P.tt("pool", PA[0][:, hs, :], PA[0][:, hs, :], view(strict[:], [[0, 16], [1, 64]]), ALU.mult)
        for gq in range(4):
            tps = bk[gq][:].bitcast(BF16)
            for j in range(8):
                h = gq * 8 + j
                P.transpose(tps[0:64, j * 64:(j + 1) * 64], PA[0][:, h, :], C.ident[0:64, 0:64])
            P.copy("act" if gq % 2 == 0 else "dve", PT[0][:, gq * 8:(gq + 1) * 8, :],
                   tps[0:64, 0:512].rearrange("p (j c) -> p j c", j=8))
        P.tt("pool", X[:], PA[0][:], view(I64, [[0, 32], [1, 64]]), ALU.add)
        P.copy("pool", Xb[:], X[:])
        cur = 0
        for lvl in range(5):
            nxt = 1 - cur
            for gq in range(4):
                if lvl < 4:
                    pp = bk[gq]
                    for j in range(8):
                        h = gq * 8 + j
                        P.matmul(pp[0:64, j * 64:(j + 1) * 64], PT[cur][:, h, :], PA[cur][:, h, :], start=True, stop=True)
                pt = bk[4 + gq]
                for j in range(8):
                    h = gq * 8 + j
                    P.matmul(pt[0:64, j * 64:(j + 1) * 64], PA[cur][:, h, :], PT[cur][:, h, :], start=True, stop=True)
            for gq in range(4):
                hs = slice(gq * 8, (gq + 1) * 8)
                P.copy("dve" if gq % 2 == 0 else "act", PT[nxt][:, hs, :], bk[4 + gq][0:64, :].rearrange("p (j c) -> p j c", j=8))
                if lvl < 4:
                    P.copy("act" if gq % 2 == 0 else "dve", PA[nxt][:, hs, :], bk[gq][0:64, :].rearrange("p (j c) -> p j c", j=8))
            for gq in range(4):
                px = bk[4 + gq]
                for j in range(8):
                    h = gq * 8 + j
                    P.matmul(px[0:64, j * 64:(j + 1) * 64], PT[nxt][:, h, :], Xb[:, h, :], start=True, stop=True)
            for gq in range(4):
                hs = slice(gq * 8, (gq + 1) * 8)
                P.tt("dve", X[:, hs, :], X[:, hs, :], bk[4 + gq][0:64, :].rearrange("p (j c) -> p j c", j=8), ALU.add)
                P.copy("pool", Xb[:, hs, :], X[:, hs, :])
            cur = nxt
        for h0 in range(0, 32, 6):
            hg = list(range(h0, min(32, h0 + 6)))
            rr, qq, vv = {}, {}, {}
            for h in hg:
                kh = h // 2
                pb = bk[h - h0]
                Sbt = P.tok(Sb, h)
                P.op("pe", lambda e, pb=pb, kh=kh, h=h, co=co: e.matmul(pb[0:64, 0:128], kT4[:, kh, co:co + 64], Sb[:, h, :], start=True, stop=True),
                     reads=[kT4, Sbt], writes=[pb])
                if not state_only:
                    P.op("pe", lambda e, pb=pb, kh=kh, h=h, co=co: e.matmul(pb[0:64, 128:256], qT4[:, kh, co:co + 64], Sb[:, h, :], start=True, stop=True),
                         reads=[qT4, Sbt], writes=[pb])
            for h in hg:
                pb = bk[h - h0]
                rr[h] = r_r.next()
                P.stt("dve", rr[h][:], pb[0:64, 0:128], gt[:, 2, h:h + 1], v_tm[:, h, :], ALU.mult, ALU.add)
                if not state_only:
                    qq[h] = qs_r.next()
                    P.act(qq[h][:], pb[0:64, 128:256], AF.Copy, scale=gt[:, 1, h:h + 1])
            for h in hg:
                pb = bk[h - h0]
                P.matmul(pb[0:64, 256:384], Xb[:, h, :], rr[h][:], start=True, stop=True)
            for h in hg:
                pb = bk[h - h0]
                vv[h] = vn_r.next()
                P.act(vv[h][:], pb[0:64, 256:384], AF.Copy, scale=bc[:, h:h + 1])
            for h in hg:
                pb = bk[h - h0]
                if not state_only:
                    P.matmul(pb[0:64, 384:512], QKd[:, h, :], vv[h][:], start=True, stop=True)
                P.matmul(pb[:, 0:128], kgl[:, h, :], vv[h][:], start=True, stop=True)
            for h in hg:
                pb = bk[h - h0]
                St = P.tok(S, h)
                Sbt = P.tok(Sb, h)
                ot = P.tok(o_c, h)
                if not state_only:
                    P.tt("dve", o_c[:, h, :], pb[0:64, 384:512], qq[h][:], ALU.add, reads=[pb, qq[h]], writes=[ot])
                P.stt("dve", S[:, h, :], S[:, h, :], gl[:, h:h + 1], pb[:, 0:128], ALU.mult, ALU.add,
                      reads=[St, gl, pb], writes=[St])
                P.copy("pool", Sb[:, h, :], S[:, h, :], reads=[St], writes=[Sbt])
        if state_only:
            continue
        P.tt("pool", tmp[:], o_c[:].rearrange("p h d -> p (h d)"), o_c[:].rearrange("p h d -> p (h d)"), ALU.mult)
        P.reduce("dve", nrm[:, 0, :], tmp[:].rearrange("p (h d) -> p h d", h=32), ALU.add)
        P.ts("dve", nrm[:, 1, :], nrm[:, 0, :], 1.0 / 128.0, RMS_EPS, ALU.mult, ALU.add)
        P.act(nrm[:, 1, :], nrm[:, 1, :], AF.Sqrt)
        P.op("dve", lambda e: e.reciprocal(nrm[:, 1, :], nrm[:, 1, :]), reads=[nrm], writes=[nrm])
        P.act(zs[:], zc[:], AF.Silu)
        P.tt("dve", o_c[:], o_c[:], view(nrm[:, 1, :], [[1, 32], [0, 128]]), ALU.mult)
        P.tt("pool", o_c[:], o_c[:], view(normw[:], [[0, 32], [1, 128]]), ALU.mult)
        P.tt("dve", on_sb[:], o_c[:].rearrange("p h d -> p (h d)"), zs[:], ALU.mult)
        for blk in range(2):
            tp = bk[6 + blk][:].bitcast(BF16)
            for j in range(16):
                h = blk * 16 + j
                P.transpose(tp[:, j * 64:(j + 1) * 64], on_sb[:, h * 128:(h + 1) * 128], C.ident[0:64, 0:64])
            P.copy("act", onT4[:, blk * 16:(blk + 1) * 16, co:co + 64], tp.rearrange("p (j t) -> p j t", j=16))
        if n % 4 == 3:
            sl = slice((n - 3) * 64, (n + 1) * 64)
            P.dma("sp", on_d[:, :, sl].rearrange("k p t -> p k t"), onT4[:])
    P.dma("sp", Sfin_d.rearrange("h p d -> p h d"), S[:])


def build_g2_prog(nchunks=16, state_only=False):
    nc = bass.Bass("TRN2", target_bir_lowering=False)
    nt = nchunks * 64
    with ExitStack() as stack:
        P = Prog(nc, stack)
        qT_d = nc.dram_tensor("qT", [16, 128, nt], BF16, kind="ExternalInput").ap()
        kT_d = nc.dram_tensor("kT", [16, 128, nt], BF16, kind="ExternalInput").ap()
        vT_d = nc.dram_tensor("vT", [32, 128, nt], BF16, kind="ExternalInput").ap()
        z_d = nc.dram_tensor("z", [nt, 4096], BF16, kind="ExternalInput").ap()
        b_d = nc.dram_tensor("beta", [nt, 32], F32, kind="ExternalInput").ap()
        g_d = nc.dram_tensor("g", [nt, 32], F32, kind="ExternalInput").ap()
        S0_d = nc.dram_tensor("S0", [32, 128, 128], F32, kind="ExternalInput").ap()
        nw_d = nc.dram_tensor("normw", [128], F32, kind="ExternalInput").ap()
        on_t = P.dram("onT", [32, 128, nt], BF16, kind="ExternalOutput")
        Sf_t = P.dram("Sfin", [32, 128, 128], F32, kind="ExternalOutput")
        C = Ctx(P, nt // 128, light=True)
        gdn_chunks(P, C, qT_d, kT_d, vT_d, z_d, b_d, g_d, S0_d, nw_d, on_t.ap(), Sf_t.ap(), nchunks=nchunks, state_only=state_only)
        P.finish(P.tok(on_t) + P.tok(Sf_t))
        P.emit()
    return nc


def rope_consts(half, nt=1024):
    inv = 1.0 / (10000.0 ** (np.arange(0, 128, 2, dtype=np.float32) / 128.0))
    pos = np.arange(half * nt, (half + 1) * nt, dtype=np.float32)
    ang = pos[None, :] * np.concatenate([inv, inv])[:, None].astype(np.float32)
    return np.cos(ang).astype(np.float32), np.sin(ang).astype(np.float32)


def rot_const():
    import ml_dtypes
    r = np.zeros((128, 128), np.float32)
    for m in range(64):
        r[m + 64, m] = -1.0
        r[m, m + 64] = 1.0
    return r.astype(ml_dtypes.bfloat16)


def build_g3_prog(ntt=8):
    nc = bass.Bass("TRN2", target_bir_lowering=False)
    nt = ntt * 128
    with ExitStack() as stack:
        P = Prog(nc, stack)
        h_in = nc.dram_tensor("h_in", [nt, D_MODEL], F32, kind="ExternalInput").ap()
        on_d = nc.dram_tensor("onT", [32, 128, nt], BF16, kind="ExternalInput").ap()
        wo_d = nc.dram_tensor("wo", [4096, D_MODEL], F32, kind="ExternalInput").ap()
        h_out_t = P.dram("h_out", [nt, D_MODEL], F32, kind="ExternalOutput")
        C = Ctx(P, ntt, light=True)
        C.h = [P.sb(f"h{t}", [128, D_MODEL], F32, ntok=4) for t in range(ntt)]
        C.xT = P.sb("xT", [128, 32, nt], BF16)
        hv = h_in.rearrange("(t p) d -> t p d", p=128)
        ov = h_out_t.ap().rearrange("(t p) d -> t p d", p=128)
        for t in range(ntt):
            P.dma("sp" if t % 2 == 0 else "act", C.h[t][:], hv[t])
        for k4 in range(4):
            P.dma("sp" if k4 % 2 == 0 else "act", C.xT[:, k4 * 8:(k4 + 1) * 8, :],
                  on_d[k4 * 8:(k4 + 1) * 8].rearrange("k p t -> p k t"))
        wo_sb = [P.sb(f"wo{i}", [128, 32, 512], BF16) for i in range(2)]
        out_proj(P, C, wo_d, wo_sb, kchunks=32)
        for t in range(ntt):
            P.dma("sp", ov[t], C.h[t][:])
        P.finish(P.tok(h_out_t))
        P.emit()
    return nc


def build_fused_prog():
    nc = bass.Bass("TRN2", target_bir_lowering=False)
    ntt, nt = 8, 1024
    groups = [[0, 1], [2, 3], [4, 5], [6, 7]]
    ext = lambda n, sh, dt=F32: nc.dram_tensor(n, list(sh), dt, kind="ExternalInput").ap()
    with ExitStack() as outer:
        P = Prog(nc, outer)
        x_d = ext("x", [nt, D_MODEL])
        f1n, f1g, f1u, f1d = ext("f1n", [2, D_MODEL]), ext("f1g", [2, D_MODEL, D_FF]), ext("f1u", [2, D_MODEL, D_FF]), ext("f1d", [2, D_FF, D_MODEL])
        f2n, f2g, f2u, f2d = ext("f2n", [2, D_MODEL]), ext("f2g", [2, D_MODEL, D_FF]), ext("f2u", [2, D_MODEL, D_FF]), ext("f2d", [2, D_FF, D_MODEL])
        mixn = ext("mixn", [2, D_MODEL])
        wqkv = ext("wqkv", [D_MODEL, 3 * D_MODEL])
        lam_d = [ext(n, [128]) for n in ("lq1", "lk1", "lq2", "lk2")]
        sub_d = ext("subln", [256])
        wo_d = ext("da_wo", [D_MODEL, D_MODEL])
        w_in = ext("w_in", [D_MODEL, 12352])
        convw = ext("conv_w", [4, 8192])
        alog, dtb, gnorm = ext("a_log", [32]), ext("dt_bias", [32]), ext("gnorm", [128])
        gwo_d = ext("g_wo", [4096, D_MODEL])
        fin_d = ext("fin", [D_MODEL])
        cos_d, sin_d = ext("cosT", [128, nt]), ext("sinT", [128, nt])
        rot_d = ext("rotT", [128, 128], BF16)
        pb_d, fl_d = ext("pbias", [128, 1]), ext("pflag", [128, 1])
        out_t = P.dram("out", [nt, D_MODEL], F32, kind="ExternalOutput")
        qT_t = P.dram("s_qT", [2048, nt], BF16)
        kT_t = [P.dram(f"s_kT{i}", [1024, nt], BF16) for i in range(2)]
        v_t = [P.dram(f"s_v{i}", [nt // 2, D_MODEL], BF16) for i in range(2)]
        kTp_t = [P.dram(f"s_kTpair{i}", [2048, nt], BF16) for i in range(2)]
        vp_t = [P.dram(f"s_vpair{i}", [nt, D_MODEL], BF16) for i in range(2)]
        hsp_t = P.dram("s_hspill", [nt, D_MODEL], F32)
        hal_t = P.dram("s_halo", [8, D_MODEL], F32)
        halp_t = P.dram("s_halopair", [16, D_MODEL], F32)
        gq_t = P.dram("s_gq", [2048, nt], BF16)
        gk_t = P.dram("s_gk", [2048, nt], BF16)
        gv_t = P.dram("s_gv", [4096, nt], BF16)
        gz_t = P.dram("s_gz", [nt, 4096], BF16)
        gb_t = P.dram("s_gb", [nt, 32], F32)
        gg_t = P.dram("s_gg", [nt, 32], F32)
        on_t = P.dram("s_on", [4096, nt], BF16)
        on2_t = P.dram("s_on2", [4096, nt], BF16)
        sf_t = P.dram("s_sf", [4096, 128], F32)
        sf2_t = P.dram("s_sf2", [4096, 128], F32)
        sp_t = P.dram("s_spair", [8192, 128], F32)
        v3 = lambda t, k: t.ap().rearrange("(s p) t -> s p t", p=128)
        C = Ctx(P, ntt, light=True)
        P.phase = 1
        hv = x_d.rearrange("(t p) d -> t p d", p=128)
        ov = out_t.ap().rearrange("(t p) d -> t p d", p=128)
        spv = hsp_t.ap().rearrange("(t p) d -> t p d", p=128)

        def phase(fn):
            with ExitStack() as ps:
                P.stack = ps
                fn()
                P.end_phase()
            P.stack = outer

        with ExitStack() as hstack:
            P.stack = hstack
            C.h = [P.sb(f"h{t}", [128, D_MODEL], F32, ntok=4) for t in range(ntt)]
            for t in range(ntt):
                P.dma("sp" if t % 2 == 0 else "act", C.h[t][:], hv[t])

            def ph_ffn(wn, wg, wu, wd):
                def f():
                    alloc_norm(P, C)
                    ffn(P, C, wn, wg, wu, wd, make_wbufs(P, nt, 2, 2), G=2)
                return f

            phase(ph_ffn(f1n[0], f1g[0], f1u[0], f1d[0]))

            def ph_qkv():
                alloc_norm(P, C)
                cos_sb = P.sb("cos_sb", [128, nt], F32)
                sin_sb = P.sb("sin_sb", [128, nt], F32)
                rot_sb = P.sb("rot_sb", [128, 128], BF16)
                P.dma("sp", cos_sb[:], cos_d)
                P.dma("sp", sin_sb[:], sin_d)
                P.dma("sp", rot_sb[:], rot_d)
                wb = make_wbufs(P, nt, 2, 2, with_ffn=False)
                rmsnorm_to_xT(P, C, mixn[0], C.bank[4:8])
                kown = lambda hd: v3(kT_t[hd // 4], 8)[2 * (hd % 4):2 * (hd % 4) + 2]
                qkv_rope(P, C, wqkv, cos_sb, sin_sb, rot_sb, v3(qT_t, 16), kown, [v_t[0].ap(), v_t[1].ap()], wb)
                for i in range(2):
                    P.collective("AllGather", kT_t[i].ap(), kTp_t[i].ap(), groups)
                    P.collective("AllGather", v_t[i].ap(), vp_t[i].ap(), groups)

            phase(ph_qkv)

            def ph_att():
                alloc_norm(P, C, xT=True)
                pbias = P.sb("pbias_sb", [128, 1], F32)
                P.dma("sp", pbias[:], pb_d)
                lt = [P.sb(f"lam{i}", [128, 128], F32) for i in range(4)]
                for i in range(4):
                    P.dma("sp", lt[i][:], lam_d[i].partition_broadcast(128))
                ls = P.sb("lam_s", [128, 4], F32)
                li = 0.8 - 0.6 * math.exp(-0.3 * 0)
                P.tt("dve", lt[0][:], lt[0][:], lt[1][:], ALU.mult)
                P.tt("dve", lt[2][:], lt[2][:], lt[3][:], ALU.mult)
                P.reduce("dve", ls[:, 0:1], lt[0][:], ALU.add)
                P.reduce("dve", ls[:, 1:2], lt[2][:], ALU.add)
                P.act(ls[:, 0:2], ls[:, 0:2], AF.Exp)
                nlam = P.sb("nlam", [128, 1], F32)
                P.tt("dve", ls[:, 2:3], ls[:, 1:2], ls[:, 0:1], ALU.subtract)
                P.ts("dve", nlam[:], ls[:, 2:3], -float(li), None, ALU.add)
                sublnw = P.sb("sublnw", [128, 256], F32)
                P.dma("sp", sublnw[:], sub_d.partition_broadcast(128))
                P.ts("dve", sublnw[:], sublnw[:], 1.0 - float(li), None, ALU.mult)
                kown = lambda hd: v3(kT_t[hd // 4], 8)[2 * (hd % 4):2 * (hd % 4) + 2]
                kprev = lambda hd: kTp_t[hd // 4].ap()[0:1024, :].rearrange("(s p) t -> s p t", p=128)[2 * (hd % 4):2 * (hd % 4) + 2]
                vprev = [vp_t[i].ap()[0:nt // 2, :] for i in range(2)]
                attn_core(P, C, v3(qT_t, 16), kprev, kown, vprev, [v_t[0].ap(), v_t[1].ap()], pbias, nlam, sublnw, ntt, ntt)
                wo_sb = [P.sb(f"wo{i}", [128, 16, 512], BF16) for i in range(2)]
                out_proj(P, C, wo_d, wo_sb)

            phase(ph_att)
            phase(ph_ffn(f2n[0], f2g[0], f2u[0], f2d[0]))
            phase(ph_ffn(f1n[1], f1g[1], f1u[1], f1d[1]))

            def ph_spill():
                for t in range(ntt):
                    P.dma("sp" if t % 2 == 0 else "act", spv[t], C.h[t][:])
                P.dma("sp", hal_t.ap()[0:3, :], C.h[ntt - 1][125:128, :])
                P.collective("AllGather", hal_t.ap(), halp_t.ap(), groups)

            phase(ph_spill)
            P.stack = outer
        def ph_g1():
            C.h = [P.sb(f"hs{i}", [128, D_MODEL], F32, ntok=4) for i in range(2)]
            C.h = [C.h[t % 2] for t in range(ntt)]
            alloc_norm(P, C)
            hh = P.sb("hh", [128, D_MODEL], F32)
            xTh = P.sb("xTh", [128, 16, 128], BF16)
            flag = P.sb("flag_sb", [128, 1], F32)
            P.dma("sp", flag[:], fl_d)
            P.memset("pool", hh[:], 0.0)
            P.dma("sp", hh[0:3, :], halp_t.ap()[0:3, :])
            P.ts("dve", hh[:], hh[:], flag[:, 0:1], None, ALU.mult)
            wb = make_wbufs(P, nt, 2, 2, with_ffn=False)
            global zw_tiles
            zw_tiles = [P.sb(f"g_wz{i}", [128, 16, 512], BF16) for i in range(2)]
            rmsnorm_stream(P, C, mixn[1], C.bank[4:8], spv, (hh, xTh))
            gdn_inproj(P, C, xTh, w_in, convw, alog, dtb, v3(gq_t, 16), v3(gk_t, 16), v3(gv_t, 32), gz_t.ap(), gb_t.ap(), gg_t.ap(),
                       wb, halo0=0)

        phase(ph_g1)

        def ph_g2a():
            gdn_chunks(P, C, v3(gq_t, 16), v3(gk_t, 16), v3(gv_t, 32), gz_t.ap(), gb_t.ap(), gg_t.ap(), None, gnorm,
                       v3(on_t, 32), sf_t.ap().rearrange("(h p) d -> h p d", p=128), state_only=True)
            P.collective("AllGather", sf_t.ap(), sp_t.ap(), groups)

        phase(ph_g2a)

        def ph_g2b():
            flag = P.sb("flag_sb", [128, 1], F32)
            P.dma("sp", flag[:], fl_d)
            gdn_chunks(P, C, v3(gq_t, 16), v3(gk_t, 16), v3(gv_t, 32), gz_t.ap(), gb_t.ap(), gg_t.ap(),
                       sp_t.ap()[0:4096, :].rearrange("(h p) d -> h p d", p=128), gnorm,
                       v3(on2_t, 32), sf2_t.ap().rearrange("(h p) d -> h p d", p=128), s0_flag=flag)

        phase(ph_g2b)
        with ExitStack() as hstack:
            P.stack = hstack
            C.h = [P.sb(f"h{t}", [128, D_MODEL], F32, ntok=4) for t in range(ntt)]
            for t in range(ntt):
                P.dma("sp" if t % 2 == 0 else "act", C.h[t][:], spv[t])

            def ph_g3():
                C.xT = P.sb("xT", [128, 32, nt], BF16)
                on3 = v3(on2_t, 32)
                for k4 in range(4):
                    P.dma("sp" if k4 % 2 == 0 else "act", C.xT[:, k4 * 8:(k4 + 1) * 8, :],
                          on3[k4 * 8:(k4 + 1) * 8].rearrange("k p t -> p k t"))
                wo_sb = [P.sb(f"wo{i}", [128, 32, 512], BF16) for i in range(2)]
                out_proj(P, C, gwo_d, wo_sb, kchunks=32)

            phase(ph_g3)

            def ph_last():
                alloc_norm(P, C)
                ffn(P, C, f2n[1], f2g[1], f2u[1], f2d[1], make_wbufs(P, nt, 2, 2), G=2)
                final_norm(P, C, fin_d, ov)
                P.finish(P.tok(out_t))

            phase(ph_last)
            P.stack = outer
    return nc


_PROGS = {}
_DEBUG = None


def _prog(name, fn):
    if name not in _PROGS:
        _PROGS[name] = fn()
    return _PROGS[name]


def _launch(name, nc, in_maps):
    res = run_bass_kernel_spmd(nc, in_maps, core_ids=list(range(8)))
    if _DEBUG is not None:
        _DEBUG[name] = res.results
    return res.results


def _c(a):
    return np.ascontiguousarray(a)


def kernel(x, ffn1_norm, ffn1_w_gate, ffn1_w_up, ffn1_w_down, mix_norm,
           ffn2_norm, ffn2_w_gate, ffn2_w_up, ffn2_w_down,
           da_w_qkv, da_lambda_q1, da_lambda_k1, da_lambda_q2, da_lambda_k2, da_subln, da_w_o,
           gdn_w_in, gdn_conv_w, gdn_a_log, gdn_dt_bias, gdn_norm, gdn_w_o, final_norm):
    f32 = np.float32
    nt = 1024
    a = lambda v: _c(np.asarray(v, f32))
    nc = _prog("fused", build_fused_prog)
    xs = a(x).reshape(-1, D_MODEL)
    shared = {
        "f1n": a(ffn1_norm), "f1g": a(ffn1_w_gate), "f1u": a(ffn1_w_up), "f1d": a(ffn1_w_down),
        "f2n": a(ffn2_norm), "f2g": a(ffn2_w_gate), "f2u": a(ffn2_w_up), "f2d": a(ffn2_w_down),
        "mixn": a(mix_norm), "wqkv": a(da_w_qkv[0]),
        "lq1": a(da_lambda_q1[0]), "lk1": a(da_lambda_k1[0]), "lq2": a(da_lambda_q2[0]), "lk2": a(da_lambda_k2[0]),
        "subln": a(da_subln[0]), "da_wo": a(da_w_o[0]),
        "w_in": a(gdn_w_in[0]), "conv_w": a(gdn_conv_w[0]), "a_log": a(gdn_a_log[0]), "dt_bias": a(gdn_dt_bias[0]),
        "gnorm": a(gdn_norm[0]), "g_wo": a(gdn_w_o[0]), "fin": a(final_norm), "rotT": rot_const(),
    }
    maps = []
    for c in range(8):
        odd = c % 2 == 1
        cs, sn = rope_consts(c % 2)
        m = dict(shared)
        m.update({"x": _c(xs[c * nt:(c + 1) * nt]), "cosT": cs, "sinT": sn,
                  "pbias": np.full((128, 1), 0.0 if odd else -30000.0, f32),
                  "pflag": np.full((128, 1), 1.0 if odd else 0.0, f32)})
        maps.append(m)
    r = _launch("fused", nc, maps)
    return np.concatenate([r[c]["out"] for c in range(8)], axis=0).reshape(BATCH, SEQ, D_MODEL).astype(f32)


def kernel_unfused(x, ffn1_norm, ffn1_w_gate, ffn1_w_up, ffn1_w_down, mix_norm,
           ffn2_norm, ffn2_w_gate, ffn2_w_up, ffn2_w_down,
           da_w_qkv, da_lambda_q1, da_lambda_k1, da_lambda_q2, da_lambda_k2, da_subln, da_w_o,
           gdn_w_in, gdn_conv_w, gdn_a_log, gdn_dt_bias, gdn_norm, gdn_w_o, final_norm):
    f32 = np.float32
    nt = 1024
    hs = [_c(np.asarray(x, f32).reshape(-1, D_MODEL)[c * nt:(c + 1) * nt]) for c in range(8)]

    def run_ffn(hs, wn, wg, wu, wd, wf=None):
        if wf is None:
            nc = _prog("ffn", lambda: build_ffn_prog(8, 2, 2, False))
        else:
            nc = _prog("ffn_final", lambda: build_ffn_prog(8, 2, 2, True))
        wn, wg, wu, wd = (_c(np.asarray(a, f32)) for a in (wn, wg, wu, wd))
        maps = []
        for c in range(8):
            m = {"h_in": hs[c], "wn": wn, "wg": wg, "wu": wu, "wd": wd}
            if wf is not None:
                m["wf"] = _c(np.asarray(wf, f32))
            maps.append(m)
        r = _launch("ffn", nc, maps)
        return [r[c]["h_out"] for c in range(8)]

    hs = run_ffn(hs, ffn1_norm[0], ffn1_w_gate[0], ffn1_w_up[0], ffn1_w_down[0])
    nc = _prog("qkv", build_qkv_prog)
    rot = rot_const()
    maps = []
    for c in range(8):
        cs, sn = rope_consts(c % 2)
        maps.append({"h_in": hs[c], "wn": _c(np.asarray(mix_norm[0], f32)), "wqkv": _c(np.asarray(da_w_qkv[0], f32)),
                     "cosT": cs, "sinT": sn, "rotT": rot})
    r = _launch("qkv", nc, maps)
    nc = _prog("att", lambda: build_att_prog(8, 0.8 - 0.6 * math.exp(-0.3 * 0)))
    maps = []
    for c in range(8):
        odd = c % 2 == 1
        maps.append({
            "h_in": hs[c], "qT": r[c]["qT"],
            "kTp": r[c - 1]["kT"] if odd else np.zeros_like(r[c]["kT"]),
            "kTo": r[c]["kT"],
            "vp": r[c - 1]["v"] if odd else np.zeros_like(r[c]["v"]),
            "vo": r[c]["v"],
            "pbias": np.full((128, 1), 0.0 if odd else -30000.0, f32),
            "lq1": _c(np.asarray(da_lambda_q1[0], f32)), "lk1": _c(np.asarray(da_lambda_k1[0], f32)),
            "lq2": _c(np.asarray(da_lambda_q2[0], f32)), "lk2": _c(np.asarray(da_lambda_k2[0], f32)),
            "subln": _c(np.asarray(da_subln[0], f32)), "wo": _c(np.asarray(da_w_o[0], f32)),
        })
    r = _launch("att", nc, maps)
    hs = [r[c]["h_out"] for c in range(8)]
    hs = run_ffn(hs, ffn2_norm[0], ffn2_w_gate[0], ffn2_w_up[0], ffn2_w_down[0])
    hs = run_ffn(hs, ffn1_norm[1], ffn1_w_gate[1], ffn1_w_up[1], ffn1_w_down[1])
    nc = _prog("g1", build_g1_prog)
    maps = []
    for c in range(8):
        halo = np.zeros((128, D_MODEL), f32)
        if c % 2 == 1:
            halo[125:128] = hs[c - 1][nt - 3:nt]
        maps.append({"h_in": hs[c], "h_halo": halo, "wn": _c(np.asarray(mix_norm[1], f32)),
                     "w_in": _c(np.asarray(gdn_w_in[0], f32)), "conv_w": _c(np.asarray(gdn_conv_w[0], f32)),
                     "a_log": _c(np.asarray(gdn_a_log[0], f32)), "dt_bias": _c(np.asarray(gdn_dt_bias[0], f32))})
    g1 = _launch("g1", nc, maps)
    nc = _prog("g2", build_g2_prog)
    nw = _c(np.asarray(gdn_norm[0], f32))

    def g2_maps(S0s):
        return [{"qT": g1[c]["qT"], "kT": g1[c]["kT"], "vT": g1[c]["vT"], "z": g1[c]["z"], "beta": g1[c]["beta"],
                 "g": g1[c]["g"], "S0": S0s[c], "normw": nw} for c in range(8)]

    zero_S = np.zeros((32, 128, 128), f32)
    ra = _launch("g2a", nc, g2_maps([zero_S] * 8))
    rb = _launch("g2b", nc, g2_maps([ra[c - 1]["Sfin"] if c % 2 == 1 else zero_S for c in range(8)]))
    onT = [rb[c]["onT"] if c % 2 == 1 else ra[c]["onT"] for c in range(8)]
    nc = _prog("g3", build_g3_prog)
    r = _launch("g3", nc, [{"h_in": hs[c], "onT": onT[c], "wo": _c(np.asarray(gdn_w_o[0], f32))} for c in range(8)])
    hs = [r[c]["h_out"] for c in range(8)]
    hs = run_ffn(hs, ffn2_norm[1], ffn2_w_gate[1], ffn2_w_up[1], ffn2_w_down[1], wf=final_norm)
    out = np.concatenate(hs, axis=0).reshape(BATCH, SEQ, D_MODEL).astype(f32)
    return out
```
